# Optimizing a Trainium2 kernel written in Bass

```python
import jax, jax.numpy as jnp
from jax import lax
import numpy as np

D_MODEL = 2048
BATCH = 8
SEQ = 2048
DEPTH = 1

HEAD_DIM = 64
N_Q_HEADS = 16
N_KV_HEADS = 4
GQA_GROUP = N_Q_HEADS // N_KV_HEADS
WINDOW = 128
ATTN_BLOCK = 128
ATTN_WIDTH = N_Q_HEADS * HEAD_DIM
KV_WIDTH = N_KV_HEADS * HEAD_DIM
SGU_GROUPS = 16
SGU_CHUNK = 128
SGU_WIDTH = 1024
SGU_GROUP_DIM = SGU_WIDTH // SGU_GROUPS
MIX_IN_WIDTH = ATTN_WIDTH + 2 * KV_WIDTH + 2 * SGU_WIDTH
N_BRANCHES = 2
D_FF = 5632
EPS = 1e-6
NEG_INF = -1e30

kernel_name = "hybrid_swa_sgu_macaron_block"


def alibi_slopes(n_heads):
    return np.asarray([2.0 ** (-8.0 * (i + 1) / n_heads) for i in range(n_heads)], dtype=np.float32)


def rmsnorm(x, g):
    xf = x.astype(jnp.float32)
    y = xf * lax.rsqrt(jnp.mean(xf * xf, axis=-1, keepdims=True) + EPS)
    return (y * g.astype(jnp.float32)).astype(x.dtype)


def layernorm(x, g, b):
    xf = x.astype(jnp.float32)
    mu = jnp.mean(xf, axis=-1, keepdims=True)
    xc = xf - mu
    y = xc * lax.rsqrt(jnp.mean(xc * xc, axis=-1, keepdims=True) + EPS)
    return (y * g.astype(jnp.float32) + b.astype(jnp.float32)).astype(x.dtype)


def swiglu(x, w_gate, w_up, w_down):
    return (jax.nn.silu(x @ w_gate) * (x @ w_up)) @ w_down


def sliding_window_attention(q, k, v, sinks):
    B, S = q.shape[0], q.shape[1]
    nb = S // ATTN_BLOCK
    qb = q.reshape(B, nb, ATTN_BLOCK, N_KV_HEADS, GQA_GROUP, HEAD_DIM)

    def band(t):
        prev = jnp.pad(t, ((0, 0), (ATTN_BLOCK, 0), (0, 0), (0, 0)))[:, :S]
        prev = prev.reshape(B, nb, ATTN_BLOCK, N_KV_HEADS, HEAD_DIM)
        cur = t.reshape(B, nb, ATTN_BLOCK, N_KV_HEADS, HEAD_DIM)
        return jnp.concatenate([prev, cur], axis=2)

    kb, vb = band(k), band(v)
    scores = jnp.einsum('bnqhgd,bnkhd->bnhgqk', qb, kb).astype(jnp.float32) * (HEAD_DIM ** -0.5)

    qpos = jnp.arange(ATTN_BLOCK)[:, None] + ATTN_BLOCK
    kpos = jnp.arange(2 * ATTN_BLOCK)[None, :]
    dist = (qpos - kpos).astype(jnp.float32)
    blk = jnp.arange(nb)[:, None, None]
    valid = (dist >= 0) & (dist < WINDOW) & (blk * ATTN_BLOCK + kpos[None] - ATTN_BLOCK >= 0)

    slopes = jnp.asarray(alibi_slopes(N_Q_HEADS)).reshape(N_KV_HEADS, GQA_GROUP)
    scores = scores - slopes[None, None, :, :, None, None] * jnp.abs(dist)[None, None, None, None]
    scores = jnp.where(valid[None, :, None, None], scores, NEG_INF)

    sink = sinks.astype(jnp.float32).reshape(N_KV_HEADS, GQA_GROUP)
    sink = jnp.broadcast_to(sink[None, None, :, :, None, None], scores.shape[:-1] + (1,))
    probs = jax.nn.softmax(jnp.concatenate([scores, sink], axis=-1), axis=-1)[..., :-1]
    out = jnp.einsum('bnhgqk,bnkhd->bnqhgd', probs.astype(v.dtype), vb)
    return out.reshape(B, S, ATTN_WIDTH)


def chunked_spatial_gating(u, z, ln_g, ln_b, w_s, b_s):
    B, S = u.shape[0], u.shape[1]
    nc = S // SGU_CHUNK
    z = layernorm(z, ln_g, ln_b).reshape(B, nc, SGU_CHUNK, SGU_GROUPS, SGU_GROUP_DIM)
    causal = jnp.tril(jnp.ones((SGU_CHUNK, SGU_CHUNK), dtype=bool))
    ws = jnp.where(causal[None], w_s, 0.0).astype(z.dtype)
    mixed = jnp.einsum('gts,bnsgc->bntgc', ws, z) + b_s.T[None, None, :, :, None].astype(z.dtype)
    return u * mixed.reshape(B, S, SGU_WIDTH)


def setup_inputs(seed: int = 0) -> dict:
    key = jax.random.key(seed)
    ks = jax.random.split(key, 24)
    L, D = DEPTH, D_MODEL

    def w(k, shape, fan_in, scale=1.0):
        return jax.random.normal(k, shape, dtype=jnp.float32) * (scale * fan_in ** -0.5)

    def gain(k, shape):
        return 1.0 + 0.05 * jax.random.normal(k, shape, dtype=jnp.float32)

    return {
        "x": jax.random.normal(ks[0], (BATCH, SEQ, D), dtype=jnp.float32),
        "ffn1_norm": gain(ks[1], (L, D)),
        "ffn1_w_gate": w(ks[2], (L, D, D_FF), D),
        "ffn1_w_up": w(ks[3], (L, D, D_FF), D),
        "ffn1_w_down": w(ks[4], (L, D_FF, D), D_FF),
        "mix_norm": gain(ks[5], (L, D)),
        "w_in": w(ks[6], (L, D, MIX_IN_WIDTH), D),
        "attn_sinks": 0.5 * jax.random.normal(ks[7], (L, N_Q_HEADS), dtype=jnp.float32),
        "sgu_norm_g": gain(ks[8], (L, SGU_WIDTH)),
        "sgu_norm_b": 0.02 * jax.random.normal(ks[9], (L, SGU_WIDTH), dtype=jnp.float32),
        "sgu_w_s": w(ks[10], (L, SGU_GROUPS, SGU_CHUNK, SGU_CHUNK), SGU_CHUNK, 0.5),
        "sgu_b_s": 1.0 + 0.1 * jax.random.normal(ks[11], (L, SGU_GROUPS, SGU_CHUNK), dtype=jnp.float32),
        "w_proj_attn": w(ks[12], (L, ATTN_WIDTH, D), ATTN_WIDTH),
        "w_proj_sgu": w(ks[13], (L, SGU_WIDTH, D), SGU_WIDTH),
        "w_branch_gate": w(ks[14], (L, D, N_BRANCHES * D), D),
        "b_branch_gate": 0.02 * jax.random.normal(ks[15], (L, N_BRANCHES * D), dtype=jnp.float32),
        "w_out": w(ks[16], (L, D, D), D),
        "ffn2_norm": gain(ks[17], (L, D)),
        "ffn2_w_gate": w(ks[18], (L, D, D_FF), D),
        "ffn2_w_up": w(ks[19], (L, D, D_FF), D),
        "ffn2_w_down": w(ks[20], (L, D_FF, D), D_FF),
        "final_norm": gain(ks[21], (D,)),
    }


def reference(x, ffn1_norm, ffn1_w_gate, ffn1_w_up, ffn1_w_down, mix_norm, w_in, attn_sinks,
              sgu_norm_g, sgu_norm_b, sgu_w_s, sgu_b_s, w_proj_attn, w_proj_sgu,
              w_branch_gate, b_branch_gate, w_out, ffn2_norm, ffn2_w_gate, ffn2_w_up,
              ffn2_w_down, final_norm):
    B, S = x.shape[0], x.shape[1]
    splits = np.cumsum([ATTN_WIDTH, KV_WIDTH, KV_WIDTH, SGU_WIDTH]).tolist()
    for l in range(DEPTH):
        x = x + 0.5 * swiglu(rmsnorm(x, ffn1_norm[l]), ffn1_w_gate[l], ffn1_w_up[l], ffn1_w_down[l])

        h = rmsnorm(x, mix_norm[l])
        proj = h @ w_in[l]
        q, k, v, u, z = jnp.split(proj, splits, axis=-1)
        attn = sliding_window_attention(
            q.reshape(B, S, N_Q_HEADS, HEAD_DIM),
            k.reshape(B, S, N_KV_HEADS, HEAD_DIM),
            v.reshape(B, S, N_KV_HEADS, HEAD_DIM),
            attn_sinks[l])
        sgu = chunked_spatial_gating(jax.nn.gelu(u), jax.nn.gelu(z), sgu_norm_g[l], sgu_norm_b[l],
                                     sgu_w_s[l], sgu_b_s[l])
        gates = jax.nn.sigmoid(h @ w_branch_gate[l] + b_branch_gate[l])
        g_attn, g_sgu = jnp.split(gates, N_BRANCHES, axis=-1)
        merged = g_attn * (attn @ w_proj_attn[l]) + g_sgu * (sgu @ w_proj_sgu[l])
        x = x + merged @ w_out[l]

        x = x + 0.5 * swiglu(rmsnorm(x, ffn2_norm[l]), ffn2_w_gate[l], ffn2_w_up[l], ffn2_w_down[l])
    return rmsnorm(x, final_norm)
```

```python
import math
from contextlib import ExitStack

import numpy as np
import concourse.bass as bass
import concourse.mybir as mybir
from concourse.bass_utils import run_bass_kernel_spmd

F32 = mybir.dt.float32
BF16 = mybir.dt.bfloat16
AF = mybir.ActivationFunctionType
ALU = mybir.AluOpType
AX = mybir.AxisListType

D = 2048
S = 2048
DFF = 5632
NCORES = 8
KD = D // 128
KF = DFF // 128
EPS = 1e-6
T = 512
NQ = T // 128
NT = S // T
NSLOT = 7
SLOPES = [2.0 ** (-8.0 * (i + 1) / 16) for i in range(16)]

ENGS = ("pe", "act", "dve", "pool", "sp")


class Op:
    __slots__ = ("eng", "fn", "deps", "signal", "count", "dma_key", "idx")

    def __init__(self, eng, fn, dma_key=None):
        self.eng = eng
        self.fn = fn
        self.deps = []
        self.signal = False
        self.count = 0
        self.dma_key = dma_key
        self.idx = -1


class Sched:
    def __init__(self, nc):
        self.nc = nc
        self.n = 0
        self.by_eng = {e: [] for e in ENGS}
        self.last_writer = {}
        self.readers = {}

    def op(self, eng, fn, reads=(), writes=(), dma_key=None):
        o = Op(eng, fn, dma_key)
        o.idx = self.n
        self.n += 1
        deps = {}
        for r in reads:
            w = self.last_writer.get(r)
            if w is not None:
                deps[w.idx] = (w, True)
        for r in writes:
            w = self.last_writer.get(r)
            if w is not None and w.idx not in deps:
                deps[w.idx] = (w, False)
            rd = self.readers.get(r)
            if rd:
                for x in rd.values():
                    if x.idx not in deps:
                        deps[x.idx] = (x, False)
        for d, raw in deps.values():
            if d.dma_key is None and dma_key is None and d.eng == eng:
                if eng == "pe" or not raw:
                    continue
            o.deps.append(d)
            d.signal = True
        for r in reads:
            self.readers.setdefault(r, {})[eng if dma_key is None else ("dma", o.idx)] = o
        for r in writes:
            self.last_writer[r] = o
            self.readers[r] = {}
        self.by_eng[eng].append(o)
        return o

    def emit(self, final_waits=()):
        nc = self.nc
        dma_keys = []
        dma_cnt = {}
        for e in ENGS:
            c = 0
            for o in self.by_eng[e]:
                if o.dma_key is not None:
                    if o.dma_key not in dma_cnt:
                        dma_cnt[o.dma_key] = 0
                        dma_keys.append(o.dma_key)
                    dma_cnt[o.dma_key] += 16
                    o.count = dma_cnt[o.dma_key]
                elif o.signal:
                    c += 1
                    o.count = c
        with ExitStack() as es:
            sems = {e: es.enter_context(nc.semaphore("s_" + e)) for e in ENGS}
            dsems = {k: es.enter_context(nc.semaphore("d_%d" % i)) for i, k in enumerate(dma_keys)}
            block = es.enter_context(nc.Block())
            engobj = {"pe": "tensor", "act": "scalar", "dve": "vector", "pool": "gpsimd", "sp": "sync"}

            def make(e):
                def body(eng):
                    seen = {}
                    for o in self.by_eng[e]:
                        need = {}
                        for d in o.deps:
                            key = ("d", d.dma_key) if d.dma_key is not None else ("e", d.eng)
                            if d.count > seen.get(key, 0) and d.count > need.get(key, 0):
                                need[key] = d.count
                        for key, cnt in need.items():
                            s = dsems[key[1]] if key[0] == "d" else sems[key[1]]
                            eng.wait_ge(s, cnt)
                            seen[key] = cnt
                        ins = o.fn(eng)
                        if o.dma_key is not None:
                            ins.then_inc(dsems[o.dma_key], 16)
                        elif o.signal:
                            ins.then_inc(sems[e], 1)
                    if e == "sp":
                        done = set()
                        for o in final_waits:
                            if o.dma_key not in done:
                                done.add(o.dma_key)
                                eng.wait_ge(dsems[o.dma_key], dma_cnt[o.dma_key])
                return body

            for e in ENGS:
                if self.by_eng[e] or e == "sp":
                    getattr(block, engobj[e])(make(e))


def build_nc(stages=("ffn1", "mix", "ffn2"), ntiles=NT):
    nc = bass.Bass("TRN2", target_bir_lowering=False)
    Sx = Sched(nc)

    def din(name, shape):
        return nc.dram_tensor(name, list(shape), F32, kind="ExternalInput").ap()

    x_d = din("x", (S, D))
    y_d = nc.dram_tensor("y", [S, D], F32, kind="ExternalOutput").ap()
    norms_d = din("norms", (128, 4 * KD))
    bgate_d = din("bgate", (128, 32))
    sinks_d = din("sinks", (1, 16))
    lng_d = din("lng", (128, 8))
    lnb_d = din("lnb", (128, 8))
    ws_d = din("ws", (16, 128, 128))
    bs_d = din("bs", (16, 128))
    ident_d = din("ident", (128, 128))
    negdist_d = din("negdist", (128, 256))
    tril_d = din("tril", (128, 128))
    W = {}
    for nm, shp in (("f1g", (D, DFF)), ("f1u", (D, DFF)), ("f1d", (DFF, D)),
                    ("win", (D, 3584)), ("wpa", (1024, D)), ("wpb", (1024, D)),
                    ("wbg", (D, 4096)), ("wout", (D, D)),
                    ("f2g", (D, DFF)), ("f2u", (D, DFF)), ("f2d", (DFF, D))):
        W[nm] = din(nm, shp).rearrange("(k p) f -> p k f", p=128)

    xT = nc.alloc_sbuf_tensor("xT", [128, KD, T], F32)
    hT = nc.alloc_sbuf_tensor("hT", [128, KD, T], BF16)
    ring = [nc.alloc_sbuf_tensor("ring%d" % i, [128, 16, 256], BF16) for i in range(NSLOT)]
    REG_F32 = 11264
    reg = nc.alloc_sbuf_tensor("reg", [128, REG_F32], F32)
    regb = reg[:, :].bitcast(BF16)
    hid = regb.rearrange("p (f t) -> p f t", t=T)
    qz = regb[:, 0:16 * T].rearrange("p (c t) -> p c t", t=T)
    zl = regb[:, 8 * T:16 * T].rearrange("p (q c) -> p q c", c=1024)
    guT = regb[:, 16 * T:24 * T].rearrange("p (c t) -> p c t", t=T)
    ga_f = reg[:, 12 * T:12 * T + NQ * 1024]
    gz = ga_f.rearrange("p (q c) -> p q c", c=1024)
    mrg = regb[:, 24 * T:40 * T].rearrange("p (c t) -> p c t", t=T)
    stg = [reg[:, i * 2048:(i + 1) * 2048] for i in range(5)]
    NT4, NT2 = 4, 10
    tmp4 = [nc.alloc_sbuf_tensor("t4_%d" % i, [128, 1024], F32) for i in range(NT4)]
    tmp2 = [nc.alloc_sbuf_tensor("t2_%d" % i, [128, 512], F32) for i in range(NT2)]
    cnt = {"t4": 0, "t2": 0, "ps": 0, "slot": 0, "sm": 0}

    def t4():
        i = cnt["t4"] % NT4
        cnt["t4"] += 1
        return tmp4[i], ("t4", i)

    def t2():
        i = cnt["t2"] % NT2
        cnt["t2"] += 1
        return tmp2[i], ("t2", i)

    NSM = 16
    smalls = nc.alloc_sbuf_tensor("smalls", [128, NSM, 32], F32)

    def sm():
        i = cnt["sm"] % NSM
        cnt["sm"] += 1
        return smalls[:, i, :], ("sm", i)

    lnst = nc.alloc_sbuf_tensor("lnst", [128, 64], F32)
    identf = nc.alloc_sbuf_tensor("identf", [128, 128], F32)
    identb = nc.alloc_sbuf_tensor("identb", [128, 128], BF16)
    onesb = nc.alloc_sbuf_tensor("onesb", [128, 128], BF16)
    negdist = nc.alloc_sbuf_tensor("negdist_sb", [128, 256], F32)
    tril = nc.alloc_sbuf_tensor("tril_sb", [128, 128], F32)
    wsT = nc.alloc_sbuf_tensor("wsT", [128, 16, 128], BF16)
    bsb = nc.alloc_sbuf_tensor("bsb", [128, 8, 128], F32)
    norms = nc.alloc_sbuf_tensor("norms_sb", [128, 4 * KD], F32)
    bgate = nc.alloc_sbuf_tensor("bgate_sb", [128, 32], F32)
    sinkb = nc.alloc_sbuf_tensor("sinkb", [128, 16], F32)
    lngb = nc.alloc_sbuf_tensor("lng_sb", [128, 8], F32)
    lnbb = nc.alloc_sbuf_tensor("lnb_sb", [128, 8], F32)
    kT2 = nc.alloc_sbuf_tensor("kT2", [128, 4, 128 + T], BF16)
    vbuf = nc.alloc_sbuf_tensor("vbuf", [128, 1 + NQ, 256], BF16)

    psum = [nc.alloc_psum_tensor("ps%d" % i, [128, 512], F32) for i in range(8)]

    def bank():
        i = cnt["ps"] % 8
        cnt["ps"] += 1
        return psum[i], ("ps", i)

    op = Sx.op

    def load_small(dst, src, key):
        op("sp", lambda e: e.dma_start(out=dst, in_=src), writes=[key], dma_key=("c", key))

    load_small(identf[:], ident_d[:, :], "identf")
    load_small(norms[:], norms_d[:, :], "norms")
    op("dve", lambda e: e.tensor_copy(out=identb[:], in_=identf[:]), reads=["identf"], writes=["identb"])
    op("dve", lambda e: e.memset(onesb[:], 1.0), writes=["onesb"])
    cst = nc.alloc_sbuf_tensor("cst", [128, 8], F32)
    op("dve", lambda e: e.memset(cst[:], 1.0), writes=["cst"])

    def preload_ln():
        op("act", lambda e: e.activation(out=cst[:, 4:8], in_=cst[:, 0:4], func=AF.Ln), reads=["cst"], writes=["dmy"])

    def setup_mixer_consts():
        load_small(negdist[:], negdist_d[:, :], "negdist")
        load_small(tril[:], tril_d[:, :], "tril")
        load_small(bgate[:], bgate_d[:, :], "bgate")
        load_small(sinkb[:], sinks_d[0:1, :].partition_broadcast(128), "sinkb")
        load_small(lngb[:], lng_d[:, :], "lngb")
        load_small(lnbb[:], lnb_d[:, :], "lnbb")
        for g in range(16):
            op("sp", lambda e, g=g: e.dma_start(out=bsb[(g % 2) * 64:(g % 2 + 1) * 64, g // 2, :],
                                                 in_=bs_d[g:g + 1, :].partition_broadcast(64)),
               writes=["bsb"], dma_key=("c", "bsb"))
        op("dve", lambda e: e.memset(kT2[:], 0.0), writes=[("kT2", c) for c in range(4)] + ["kT2pre"])
        op("dve", lambda e: e.memset(vbuf[:], 0.0), writes=[("vb", q) for q in range(NQ + 1)])
        for half in range(4):
            buf, bk = t4()
            b3 = buf[:, :].rearrange("p (g s) -> p g s", s=128)
            op("sp", lambda e, half=half, b3=b3: e.dma_start(
                out=b3[:, 0:4, :], in_=ws_d[half * 4:(half + 1) * 4, :, :].rearrange("g t s -> t g s")),
               writes=[bk], dma_key=("c", "ws%d" % half))
            mk_buf, mk = t2()
            m3 = mk_buf[:, :].bitcast(BF16)[:, 0:512].rearrange("p (g s) -> p g s", s=128)
            op("dve", lambda e, b3=b3, m3=m3: e.tensor_tensor(
                out=m3, in0=b3[:, 0:4, :], in1=tril[:, :].unsqueeze(1).to_broadcast([128, 4, 128]), op=ALU.mult),
               reads=[bk, "tril"], writes=[mk])
            pb, pk = bank()
            pbb = pb[:, :].bitcast(BF16)
            for gi in range(4):
                op("pe", lambda e, gi=gi, m3=m3, pbb=pbb: e.transpose(
                    out=pbb[:, gi * 128:(gi + 1) * 128], in_=m3[:, gi, :], identity=identb[:]),
                   reads=[mk, "identb"], writes=[pk])
            op("dve", lambda e, half=half, pbb=pbb: e.tensor_copy(
                out=wsT[:, half * 4:(half + 1) * 4, :].rearrange("p g t -> p (g t)"), in_=pbb[:, 0:512]),
               reads=[pk], writes=["wsT"])
        for j in range(8):
            pb, pk = bank()
            for gi in range(2):
                g = 2 * j + gi
                op("pe", lambda e, pb=pb, gi=gi, g=g: e.matmul(pb[gi * 64:(gi + 1) * 64, 0:128], lhsT=onesb[:, 0:64], rhs=wsT[:, g, :],
                                                              start=True, stop=True),
                   reads=["onesb", "wsT"], writes=[pk])
            op("dve", lambda e, pb=pb, j=j: e.scalar_tensor_tensor(
                out=bsb[:, j, :], in0=pb[:, 0:128], scalar=lnbb[:, j:j + 1], in1=bsb[:, j, :], op0=ALU.mult, op1=ALU.add),
               reads=[pk, "bsb", "lnbb"], writes=["bsb2"])

    def wload(wname, k0, kc, c0, ncols=256, dup=None):
        i = cnt["slot"] % NSLOT
        cnt["slot"] += 1
        slot = ring[i]
        key = ("slot", i)
        wv = W[wname]
        if dup is None:
            op("pool", lambda e: e.dma_start(out=slot[:, 0:kc, 0:ncols], in_=wv[:, k0:k0 + kc, c0:c0 + ncols]),
               writes=[key], dma_key=key)
        else:
            for g in range(2):
                for dd in range(2):
                    dst = slot[:, 0:kc, g * 128 + dd * 64:g * 128 + dd * 64 + 64]
                    src = wv[:, k0:k0 + kc, c0 + g * 64:c0 + g * 64 + 64]
                    op("pool", lambda e, dst=dst, src=src: e.dma_start(out=dst, in_=src), writes=[key], dma_key=key)
        return slot, key

    def mm(ps_ap, lhsT, rhs, start, stop, reads, pk):
        op("pe", lambda e: e.matmul(ps_ap, lhsT=lhsT, rhs=rhs, start=start, stop=stop), reads=reads, writes=[pk])

    def xkeys():
        return [("x", k) for k in range(KD)]

    def rstd_from(src_ap, src_keys, n, mul, dst_ap, dst_key, newton=False):
        a_, ak = t2()
        b_, bk = t2()
        a = a_[:, 0:n]
        b = b_[:, 0:n]
        op("dve", lambda e: e.tensor_scalar(out=a, in0=src_ap, scalar1=mul, scalar2=EPS, op0=ALU.mult, op1=ALU.add),
           reads=src_keys, writes=[ak])
        op("act", lambda e: e.activation(out=b, in_=a, func=AF.Ln), reads=[ak], writes=[bk])
        op("act", lambda e: e.activation(out=b, in_=b, func=AF.Exp, scale=-0.5), reads=[bk], writes=[bk])
        if not newton:
            op("dve", lambda e: e.tensor_copy(out=dst_ap, in_=b), reads=[bk], writes=[dst_key])
            return
        c_, ck = t2()
        c = c_[:, 0:n]
        op("dve", lambda e: e.tensor_tensor(out=c, in0=b, in1=b, op=ALU.mult), reads=[bk], writes=[ck])
        op("dve", lambda e: e.tensor_tensor(out=c, in0=c, in1=a, op=ALU.mult), reads=[ck, ak], writes=[ck])
        op("dve", lambda e: e.tensor_scalar(out=c, in0=c, scalar1=-0.5, scalar2=1.5, op0=ALU.mult, op1=ALU.add),
           reads=[ck], writes=[ck])
        op("dve", lambda e: e.tensor_tensor(out=dst_ap, in0=b, in1=c, op=ALU.mult), reads=[bk, ck], writes=[dst_key])

    def rmsnorm(which, final=False):
        pb, pk = bank()
        for k in range(KD):
            sq_, sk = t2()
            sq = sq_[:, :].bitcast(BF16)[:, 0:T]
            op("act", lambda e, k=k, sq=sq: e.activation(out=sq, in_=xT[:, k, :], func=AF.Square),
               reads=[("x", k)], writes=[sk])
            mm(pb[:, 0:T], onesb[:, :], sq, k == 0, k == KD - 1, [sk, "onesb"], pk)
        r_, rk = t2()
        rstd = r_[:, 0:T]
        rstd_from(pb[:, 0:T], [pk], T, 1.0 / D, rstd, rk)
        for k in range(KD):
            if final:
                op("dve", lambda e, k=k: e.scalar_tensor_tensor(
                    out=xT[:, k, :], in0=xT[:, k, :], scalar=norms[:, which * KD + k:which * KD + k + 1],
                    in1=rstd, op0=ALU.mult, op1=ALU.mult),
                   reads=[("x", k), rk, "norms"], writes=[("x", k)])
            else:
                op("dve", lambda e, k=k: e.scalar_tensor_tensor(
                    out=hT[:, k, :], in0=xT[:, k, :], scalar=norms[:, which * KD + k:which * KD + k + 1],
                    in1=rstd, op0=ALU.mult, op1=ALU.mult),
                   reads=[("x", k), rk, "norms"], writes=[("h", k)])

    def ffn(pfx, which_norm):
        rmsnorm(which_norm)
        for cb in range(KF // 2):
            sg, kg = wload(pfx + "g", 0, KD, cb * 256)
            su, ku = wload(pfx + "u", 0, KD, cb * 256)
            for j in range(2):
                f = cb * 2 + j
                pg, pgk = bank()
                pu, puk = bank()
                for k in range(KD):
                    mm(pg[:, 0:T], sg[:, k, j * 128:(j + 1) * 128], hT[:, k, :], k == 0, k == KD - 1, [kg, ("h", k)], pgk)
                for k in range(KD):
                    mm(pu[:, 0:T], su[:, k, j * 128:(j + 1) * 128], hT[:, k, :], k == 0, k == KD - 1, [ku, ("h", k)], puk)
                s_, sk = t2()
                op("act", lambda e, s_=s_, pg=pg: e.activation(out=s_[:, 0:T], in_=pg[:, 0:T], func=AF.Silu),
                   reads=[pgk], writes=[sk])
                op("dve", lambda e, s_=s_, pu=pu, f=f: e.tensor_tensor(out=hid[:, f, :], in0=s_[:, 0:T], in1=pu[:, 0:T], op=ALU.mult),
                   reads=[sk, puk], writes=[("hid", f)])
        preload_ln()
        kgs = [(0, 16), (16, 16), (32, 12)]
        for dp in range(KD // 2):
            pbs = [bank() for _ in range(2)]
            for gi, (k0, kc) in enumerate(kgs):
                sd, kd = wload(pfx + "d", k0, kc, dp * 256)
                for j in range(2):
                    pb, pk = pbs[j]
                    for fl in range(kc):
                        f = k0 + fl
                        mm(pb[:, 0:T], sd[:, fl, j * 128:(j + 1) * 128], hid[:, f, :], f == 0, f == KF - 1,
                           [kd, ("hid", f)], pk)
            for j in range(2):
                d = dp * 2 + j
                pb, pk = pbs[j]
                op("dve", lambda e, pb=pb, d=d: e.scalar_tensor_tensor(
                    out=xT[:, d, :], in0=pb[:, 0:T], scalar=0.5, in1=xT[:, d, :], op0=ALU.mult, op1=ALU.add),
                   reads=[pk, ("x", d)], writes=[("x", d)])

    def UQ(c):
        return [("uq", c, q) for q in range(NQ)]

    def mixer(tile):
        U = lambda n: ("u", n)
        rmsnorm(1)
        for cb in range(4):
            sw, kw = wload("win", 0, KD, cb * 256)
            for j in range(2):
                c = cb * 2 + j
                pb, pk = bank()
                for k in range(KD):
                    mm(pb[:, 0:T], sw[:, k, j * 128:(j + 1) * 128], hT[:, k, :], k == 0, k == KD - 1, [kw, ("h", k)], pk)
                op("act", lambda e, pb=pb, c=c: e.mul(out=qz[:, c, :], in_=pb[:, 0:T], mul=0.125),
                   reads=[pk], writes=UQ(c))
        for gp in range(2):
            sw, kw = wload("win", 0, KD, 1024 + gp * 128, dup=True)
            for j in range(2):
                g = gp * 2 + j
                pb, pk = bank()
                for k in range(KD):
                    mm(pb[:, 0:T], sw[:, k, j * 128:(j + 1) * 128], hT[:, k, :], k == 0, k == KD - 1, [kw, ("h", k)], pk)
                op("dve", lambda e, pb=pb, g=g: e.tensor_copy(out=kT2[:, g, 128:128 + T], in_=pb[:, 0:T]),
                   reads=[pk], writes=[("kT2", g)])
        sw, kw = wload("win", 0, KD, 1280)
        for q in range(NQ):
            pb, pk = bank()
            for k in range(KD):
                mm(pb[:, 0:256], hT[:, k, q * 128:(q + 1) * 128], sw[:, k, :], k == 0, k == KD - 1, [kw, ("h", k)], pk)
            op("act", lambda e, pb=pb, q=q: e.copy(out=vbuf[:, 1 + q, :], in_=pb[:, 0:256]),
               reads=[pk], writes=[("vb", 1 + q)])

        fillers = []
        ustate = {}

        def u_unit(c):
            def f():
                cb, j = c // 2, c % 2
                if j == 0:
                    ustate["u"] = wload("win", 0, KD, 1536 + cb * 256)
                sw, kw = ustate["u"]
                pb, pk = bank()
                for k in range(KD):
                    mm(pb[:, 0:T], sw[:, k, j * 128:(j + 1) * 128], hT[:, k, :], k == 0, k == KD - 1, [kw, ("h", k)], pk)
                op("act", lambda e: e.activation(out=guT[:, c, :], in_=pb[:, 0:T], func=AF.Gelu_apprx_tanh),
                   reads=[pk], writes=[U(16 + c)])
            return f

        def z_unit(zs, q):
            def f():
                if q == 0:
                    ustate["z"] = wload("win", 0, KD, 2560 + zs * 256)
                sw, kw = ustate["z"]
                pb, pk = bank()
                for k in range(KD):
                    mm(pb[:, 0:256], hT[:, k, q * 128:(q + 1) * 128], sw[:, k, :], k == 0, k == KD - 1, [kw, ("h", k)], pk)
                op("act", lambda e: e.activation(out=gz[:, q, zs * 256:(zs + 1) * 256], in_=pb[:, 0:256], func=AF.Gelu_apprx_tanh),
                   reads=[pk], writes=[U(24 + 4 * q + zs)])
            return f

        def ln_stats():
            st_, stk = lnst[:, 0:32], "lnst"
            ustate["st"] = (st_, stk)
            for q in range(NQ):
                bn_, bnk = sm()
                for hh in range(2):
                    op("dve", lambda e, q=q, hh=hh, bn_=bn_: e.bn_stats(out=bn_[:, hh * 6:(hh + 1) * 6], in_=gz[:, q, hh * 512:(hh + 1) * 512]),
                       reads=[U(24 + 4 * q + 2 * hh), U(24 + 4 * q + 2 * hh + 1)], writes=[bnk])
                op("dve", lambda e, q=q, bn_=bn_: e.bn_aggr(out=st_[:, 2 * q:2 * q + 2], in_=bn_[:, 0:12].rearrange("p (a b) -> p a b", a=2)),
                   reads=[bnk], writes=[stk])
            var_ap = st_[:, 0:2 * NQ].rearrange("p (q t) -> p q t", t=2)[:, :, 1]
            rs_, rsk = lnst[:, 32:64], "lnrs"
            ustate["rs"] = (rs_, rsk)
            rstd_from(var_ap, [stk], NQ, 1.0, rs_[:, 0:NQ], rsk)

        def ln_unit(q):
            def f():
                st_, stk = ustate["st"]
                rs_, rsk = ustate["rs"]
                gk = [U(24 + 4 * q + z) for z in range(4)]
                op("dve", lambda e: e.tensor_scalar(out=zl[:, q, :], in0=gz[:, q, :], scalar1=st_[:, 2 * q:2 * q + 1],
                                                    scalar2=rs_[:, q:q + 1], op0=ALU.subtract, op1=ALU.mult),
                   reads=gk + [stk, rsk], writes=[U(8 + 2 * q), U(9 + 2 * q)])
            return f

        def sgu_unit(j):
            def f():
                pb, pk = bank()
                p3 = pb[:, 0:T].rearrange("p (q t) -> p q t", t=128)
                for q in range(NQ):
                    for gi in range(2):
                        g = 2 * j + gi
                        mm(p3[gi * 64:(gi + 1) * 64, q, :], zl[:, q, g * 64:(g + 1) * 64], wsT[:, g, :], True, True,
                           [U(8 + 2 * q), U(9 + 2 * q), "wsT"], pk)
                t_, tk = t2()
                t3 = t_[:, 0:T].rearrange("p (q t) -> p q t", t=128)
                op("dve", lambda e: e.scalar_tensor_tensor(
                    out=t3, in0=p3, scalar=lngb[:, j:j + 1], in1=bsb[:, j, :].unsqueeze(1).to_broadcast([128, NQ, 128]),
                    op0=ALU.mult, op1=ALU.add),
                   reads=[pk, "bsb2", "lngb"], writes=[tk])
                op("dve", lambda e: e.tensor_tensor(out=guT[:, j, :], in0=t_[:, 0:T], in1=guT[:, j, :], op=ALU.mult),
                   reads=[tk, U(16 + j)], writes=[U(16 + j)])
            return f

        for c in range(8):
            u_unit(c)()
        for zs in range(4):
            for q in range(NQ):
                fillers.append(z_unit(zs, q))
        fillers.append(ln_stats)
        for q in range(NQ):
            fillers.append(ln_unit(q))
        for j in range(8):
            fillers.append(sgu_unit(j))

        its = [(qb, g) for qb in range(NQ) for g in range(4)]
        ast = {}

        def att_A1(i):
            qb, g = its[i]
            psA, kA = bank()
            psB, kB = bank()
            s_, sk = t4()
            s3 = s_[:, :].rearrange("p (h k) -> p h k", k=256)
            for hh in range(4):
                hq = 4 * g + hh
                c, po = hq // 2, (hq % 2) * 64
                pp, pkk = (psA, kA) if hh % 2 == 0 else (psB, kB)
                col = (hh // 2) * 256
                mm(pp[:, col:col + 256], qz[po:po + 64, c, qb * 128:(qb + 1) * 128],
                   kT2[po:po + 64, g, qb * 128:qb * 128 + 256], True, True,
                   [("uq", c, qb), ("kT2", g), "kT2pre"], pkk)
            for hh in range(4):
                hq = 4 * g + hh
                pp, pkk = (psA, kA) if hh % 2 == 0 else (psB, kB)
                col = (hh // 2) * 256
                op("dve", lambda e, hh=hh, hq=hq, pp=pp, col=col: e.scalar_tensor_tensor(
                    out=s3[:, hh, :], in0=negdist[:, :], scalar=SLOPES[hq], in1=pp[:, col:col + 256],
                    op0=ALU.mult, op1=ALU.add),
                   reads=[pkk, "negdist"], writes=[sk])
            if tile == 0 and qb == 0:
                op("dve", lambda e: e.memset(s3[:, :, 0:128], -1e30), writes=[sk])
            m_, mk = sm()
            r_, rk = sm()
            op("dve", lambda e: e.tensor_reduce(out=m_[:, 0:4], in_=s3, axis=AX.X, op=ALU.max),
               reads=[sk], writes=[mk])
            op("dve", lambda e: e.tensor_tensor(out=m_[:, 0:4], in0=m_[:, 0:4], in1=sinkb[:, 4 * g:4 * g + 4], op=ALU.max),
               reads=[mk, "sinkb"], writes=[mk])
            op("dve", lambda e: e.tensor_scalar(out=m_[:, 4:8], in0=m_[:, 0:4], scalar1=-1.0, scalar2=None, op0=ALU.mult),
               reads=[mk], writes=[mk])
            op("dve", lambda e: e.tensor_tensor(out=r_[:, 4:8], in0=sinkb[:, 4 * g:4 * g + 4], in1=m_[:, 4:8], op=ALU.add),
               reads=[mk, "sinkb"], writes=[rk])
            op("dve", lambda e: e.memset(r_[:, 0:4], 0.0), writes=[rk])
            ast[i] = {"s3": s3, "sk": sk, "m": m_, "mk": mk, "r": r_, "rk": rk}

        def att_A2(i):
            st = ast[i]
            s3, sk, m_, mk, r_, rk = st["s3"], st["sk"], st["m"], st["mk"], st["r"], st["rk"]
            for hh in range(4):
                op("act", lambda e, hh=hh: e.activation(
                    out=s3[:, hh, :], in_=s3[:, hh, :], func=AF.Exp, bias=m_[:, 4 + hh:5 + hh], scale=1.0,
                    accum_out=r_[:, hh:hh + 1]),
                   reads=[sk, mk, rk], writes=[sk, rk])
            op("act", lambda e: e.activation(out=r_[:, 4:8], in_=r_[:, 4:8], func=AF.Exp),
               reads=[rk], writes=[rk])
            op("dve", lambda e: e.tensor_tensor(out=r_[:, 8:12], in0=r_[:, 0:4], in1=r_[:, 4:8], op=ALU.add),
               reads=[rk], writes=[rk])
            op("dve", lambda e: e.reciprocal(out=r_[:, 12:16], in_=r_[:, 8:12]), reads=[rk], writes=[rk])

        def att_A3(i):
            st = ast[i]
            s3, sk, r_, rk = st["s3"], st["sk"], st["r"], st["rk"]
            p_, pk_ = t2()
            P3 = p_[:, :].bitcast(BF16).rearrange("p (h k) -> p h k", k=256)
            for hh in range(4):
                op("act", lambda e, hh=hh: e.mul(out=P3[:, hh, :], in_=s3[:, hh, :], mul=r_[:, 12 + hh:13 + hh]),
                   reads=[sk, rk], writes=[pk_])
            st["P3"], st["pk"] = P3, pk_

        def att_B1(i):
            P3, pk_ = ast[i]["P3"], ast[i]["pk"]
            pt, ptk = bank()
            ptb = pt[:, :].bitcast(BF16).rearrange("p (kb h q) -> p kb h q", kb=2, h=4)
            for kb in range(2):
                for hh in range(4):
                    op("pe", lambda e, kb=kb, hh=hh: e.transpose(
                        out=ptb[:, kb, hh, :], in_=P3[:, hh, kb * 128:(kb + 1) * 128], identity=identb[:]),
                       reads=[pk_, "identb"], writes=[ptk])
            pts_, ptsk = t2()
            ptsb = pts_[:, :].bitcast(BF16)
            op("dve", lambda e: e.tensor_copy(out=ptsb, in_=pt[:, :].bitcast(BF16)), reads=[ptk], writes=[ptsk])
            ast[i]["pts4"] = ptsb.rearrange("p (kb h q) -> p kb h q", kb=2, h=4)
            ast[i]["ptsk"] = ptsk

        def att_B2(i):
            qb, g = its[i]
            pts4, ptsk = ast[i]["pts4"], ast[i]["ptsk"]
            po_, pok = bank()
            o3 = po_[:, 0:256].rearrange("p (hp q) -> p hp q", q=128)
            for hh in range(4):
                hp, half = hh // 2, hh % 2
                for kb in range(2):
                    mm(o3[half * 64:(half + 1) * 64, hp, :], vbuf[:, qb + kb, g * 64:(g + 1) * 64],
                       pts4[:, kb, hh, :], kb == 0, kb == 1, [("vb", qb + kb), ptsk], pok)
            op("act", lambda e: e.copy(out=qz[:, 2 * g:2 * g + 2, qb * 128:(qb + 1) * 128], in_=o3),
               reads=[pok], writes=[("uq", 2 * g, qb), ("uq", 2 * g + 1, qb)])
            del ast[i]

        nit = len(its)
        stages_att = [att_A1, att_A2, att_A3, att_B1, att_B2]
        nsteps = nit + len(stages_att) - 1
        fi = 0
        for step in range(nsteps):
            for lag, fn in enumerate(stages_att):
                i = step - lag
                if 0 <= i < nit:
                    fn(i)
            nz = 4 * NQ + 1
            if step < 4:
                want = (nz * (step + 1) + 3) // 4
            else:
                want = nz + ((len(fillers) - nz) * (step - 3) + (nsteps - 8) - 1) // max(1, nsteps - 8)
            while fi < min(want, len(fillers)):
                fillers[fi]()
                fi += 1
        while fi < len(fillers):
            fillers[fi]()
            fi += 1
        op("dve", lambda e: e.tensor_copy(out=kT2[:, :, 0:128], in_=kT2[:, :, T:T + 128]),
           reads=[("kT2", g) for g in range(4)], writes=["kT2pre"])
        op("dve", lambda e: e.tensor_copy(out=vbuf[:, 0, :], in_=vbuf[:, NQ, :]), reads=[("vb", NQ)], writes=[("vb", 0)])
        for dp in range(8):
            i = cnt["slot"] % NSLOT
            cnt["slot"] += 1
            sab, kab = ring[i], ("slot", i)
            op("pool", lambda e, sab=sab, dp=dp: e.dma_start(out=sab[:, 0:8, :], in_=W["wpa"][:, 0:8, dp * 256:(dp + 1) * 256]),
               writes=[kab], dma_key=kab)
            op("pool", lambda e, sab=sab, dp=dp: e.dma_start(out=sab[:, 8:16, :], in_=W["wpb"][:, 0:8, dp * 256:(dp + 1) * 256]),
               writes=[kab], dma_key=kab)
            sga, kga = wload("wbg", 0, KD, dp * 256)
            sgb, kgb = wload("wbg", 0, KD, 2048 + dp * 256)
            for j in range(2):
                d = dp * 2 + j
                cs = slice(j * 128, (j + 1) * 128)
                pGA, pGAk = bank()
                pGB, pGBk = bank()
                pA, pAk = bank()
                pB, pBk = bank()

                def g_groups():
                    for k in range(KD):
                        mm(pGA[:, 0:T], sga[:, k, cs], hT[:, k, :], k == 0, k == KD - 1, [kga, ("h", k)], pGAk)
                    for k in range(KD):
                        mm(pGB[:, 0:T], sgb[:, k, cs], hT[:, k, :], k == 0, k == KD - 1, [kgb, ("h", k)], pGBk)

                def p_groups():
                    for c in range(8):
                        mm(pA[:, 0:T], sab[:, c, cs], qz[:, c, :], c == 0, c == 7, [kab] + UQ(c), pAk)
                    for c in range(8):
                        mm(pB[:, 0:T], sab[:, 8 + c, cs], guT[:, c, :], c == 0, c == 7, [kab, ("u", 16 + c)], pBk)

                if j == 0:
                    g_groups()
                    p_groups()
                else:
                    p_groups()
                    g_groups()
                ga_, gak = t2()
                gb_, gbk = t2()
                op("act", lambda e, ga_=ga_, pGA=pGA, d=d: e.activation(out=ga_[:, 0:T], in_=pGA[:, 0:T], func=AF.Sigmoid, bias=bgate[:, d:d + 1], scale=1.0),
                   reads=[pGAk, "bgate"], writes=[gak])
                op("act", lambda e, gb_=gb_, pGB=pGB, d=d: e.activation(out=gb_[:, 0:T], in_=pGB[:, 0:T], func=AF.Sigmoid, bias=bgate[:, 16 + d:17 + d], scale=1.0),
                   reads=[pGBk, "bgate"], writes=[gbk])
                op("dve", lambda e, ga_=ga_, pA=pA: e.tensor_tensor(out=ga_[:, 0:T], in0=ga_[:, 0:T], in1=pA[:, 0:T], op=ALU.mult),
                   reads=[gak, pAk], writes=[gak])
                op("dve", lambda e, gb_=gb_, pB=pB: e.tensor_tensor(out=gb_[:, 0:T], in0=gb_[:, 0:T], in1=pB[:, 0:T], op=ALU.mult),
                   reads=[gbk, pBk], writes=[gbk])
                op("dve", lambda e, ga_=ga_, gb_=gb_, d=d: e.tensor_tensor(out=mrg[:, d, :], in0=ga_[:, 0:T], in1=gb_[:, 0:T], op=ALU.add),
                   reads=[gak, gbk], writes=[("u", 24 + d)])
        preload_ln()
        for dp in range(8):
            so, ko = wload("wout", 0, KD, dp * 256)
            for j in range(2):
                d = dp * 2 + j
                pb, pk = bank()
                for k in range(KD):
                    mm(pb[:, 0:T], so[:, k, j * 128:(j + 1) * 128], mrg[:, k, :], k == 0, k == KD - 1, [ko, ("u", 24 + k)], pk)
                op("dve", lambda e, pb=pb, d=d: e.tensor_tensor(out=xT[:, d, :], in0=pb[:, 0:T], in1=xT[:, d, :], op=ALU.add),
                   reads=[pk, ("x", d)], writes=[("x", d)])

    out_ops = []
    final = "final" in stages
    cin = {"n": 0}

    def stg_keys(i):
        return [("stg", i)] + [("hid", f) for f in range(8 * i, 8 * i + 8)]

    def x_load(t0, q):
        i = 2 + cin["n"] % 3
        cin["n"] += 1
        sg_ = stg[i]
        op("sp", lambda e: e.dma_start(out=sg_, in_=x_d[t0 + q * 128:t0 + (q + 1) * 128, :]),
           writes=stg_keys(i), dma_key=("stg", i))
        return i

    def x_in(q, i):
        sg_ = stg[i]
        for kk in range(KD // 4):
            pb, pk = bank()
            for j in range(4):
                k = kk * 4 + j
                op("pe", lambda e, pb=pb, j=j, k=k: e.transpose(
                    out=pb[:, j * 128:(j + 1) * 128], in_=sg_[:, k * 128:(k + 1) * 128], identity=identf[:]),
                   reads=[("stg", i), "identf"], writes=[pk])
            wk = [("x", kk * 4 + j) for j in range(4)] + [("xq", kk * 4 + j, q) for j in range(4)]
            if kk % 2 == 0:
                op("act", lambda e, pb=pb, kk=kk: e.copy(
                    out=xT[:, kk * 4:(kk + 1) * 4, q * 128:(q + 1) * 128], in_=pb[:, :].rearrange("p (i t) -> p i t", t=128)),
                   reads=[pk], writes=wk)
            else:
                op("dve", lambda e, pb=pb, kk=kk: e.tensor_copy(
                    out=xT[:, kk * 4:(kk + 1) * 4, q * 128:(q + 1) * 128], in_=pb[:, :].rearrange("p (i t) -> p i t", t=128)),
                   reads=[pk], writes=wk)

    def y_out(t0, q, rt):
        io = q % 2
        sg_ = stg[io]
        for kk in range(KD // 4):
            pb, pk = bank()
            for j in range(4):
                k = kk * 4 + j
                op("pe", lambda e, pb=pb, j=j, k=k: e.transpose(
                    out=pb[:, j * 128:(j + 1) * 128], in_=xT[:, k, q * 128:(q + 1) * 128], identity=identf[:]),
                   reads=[("xq", k, q), "identf"], writes=[pk])
            if rt is not None:
                rt_, rtk = rt
                if kk % 2 == 0:
                    op("act", lambda e, pb=pb, kk=kk: e.mul(out=sg_[:, kk * 512:(kk + 1) * 512], in_=pb[:, :], mul=rt_[:, q:q + 1]),
                       reads=[pk, rtk], writes=stg_keys(io))
                else:
                    op("dve", lambda e, pb=pb, kk=kk: e.tensor_scalar(out=sg_[:, kk * 512:(kk + 1) * 512], in0=pb[:, :],
                                                                    scalar1=rt_[:, q:q + 1], scalar2=None, op0=ALU.mult),
                       reads=[pk, rtk], writes=stg_keys(io))
            elif kk % 2 == 0:
                op("act", lambda e, pb=pb, kk=kk: e.copy(out=sg_[:, kk * 512:(kk + 1) * 512], in_=pb[:, :]),
                   reads=[pk], writes=stg_keys(io))
            else:
                op("dve", lambda e, pb=pb, kk=kk: e.tensor_copy(out=sg_[:, kk * 512:(kk + 1) * 512], in_=pb[:, :]),
                   reads=[pk], writes=stg_keys(io))
        o = op("sp", lambda e: e.dma_start(out=y_d[t0 + q * 128:t0 + (q + 1) * 128, :], in_=sg_),
               reads=stg_keys(io), dma_key=("out", io))
        out_ops.append(o)

    pend = [x_load(0, q) for q in range(min(3, NQ))]
    for q in range(NQ):
        if q + 3 < NQ:
            pass
        x_in(q, pend[q])
        if q + 3 < NQ:
            pend.append(x_load(0, q + 3))
    for tile in range(ntiles):
        t0 = tile * T
        if "ffn1" in stages:
            ffn("f1", 0)
        if "mix" in stages:
            if tile == 0:
                setup_mixer_consts()
            mixer(tile)
        if "ffn2" in stages:
            ffn("f2", 2)
        rt = None
        if final:
            pst, pstk = bank()
            for k in range(KD):
                sq_, sk = t2()
                sq = sq_[:, :].bitcast(BF16)[:, 0:T]
                op("act", lambda e, k=k, sq=sq: e.activation(out=sq, in_=xT[:, k, :], func=AF.Square),
                   reads=[("x", k)], writes=[sk])
                for q in range(NQ):
                    op("pe", lambda e, k=k, q=q, sq=sq, pst=pst: e.matmul(
                        pst[:, q:q + 1], lhsT=sq[:, q * 128:(q + 1) * 128], rhs=onesb[:, 0:1],
                        start=(k == 0 and q == 0), stop=(k == KD - 1), skip_group_check=True),
                       reads=[sk, "onesb"], writes=[pstk])
                op("dve", lambda e, k=k: e.tensor_scalar(out=xT[:, k, :], in0=xT[:, k, :], scalar1=norms[:, 3 * KD + k:3 * KD + k + 1],
                                                         scalar2=None, op0=ALU.mult),
                   reads=[("x", k), "norms"], writes=[("x", k)] + [("xq", k, q) for q in range(NQ)])
            rt_, rtk = sm()
            rstd_from(pst[:, 0:NQ], [pstk], NQ, 1.0 / D, rt_[:, 0:NQ], rtk)
            rt = (rt_, rtk)
        else:
            for k in range(KD):
                op("dve", lambda e, k=k: e.tensor_copy(out=xT[:, k, 0:1], in_=xT[:, k, 0:1]),
                   reads=[("x", k)], writes=[("x", k)] + [("xq", k, q) for q in range(NQ)])
        nxt = tile + 1 < ntiles
        t1 = (tile + 1) * T
        pend = [x_load(t1, q) for q in range(min(3, NQ))] if nxt else []
        for q in range(NQ):
            y_out(t0, q, rt)
            if nxt and q >= 1:
                qi = q - 1
                x_in(qi, pend[qi])
                if qi + 3 < NQ:
                    pend.append(x_load(t1, qi + 3))
        if nxt:
            qi = NQ - 1
            x_in(qi, pend[qi])
    Sx.emit(final_waits=out_ops)
    return nc


_NC_CACHE = {}


def _consts():
    i = np.arange(128, dtype=np.float64)[:, None]
    j = np.arange(256, dtype=np.float64)[None, :]
    dist = 128.0 + i - j
    valid = (dist >= 0) & (dist < 128)
    negdist = np.where(valid, -dist, -1e32).astype(np.float32)
    t = np.arange(128)[:, None]
    s = np.arange(128)[None, :]
    tril = (s <= t).astype(np.float32)
    return np.eye(128, dtype=np.float32), negdist, tril


def _pk(v):
    return np.ascontiguousarray(np.asarray(v, dtype=np.float32).reshape(-1, 128).T)


def make_in_maps(inputs):
    ident, negdist, tril = _consts()
    f = lambda a: np.ascontiguousarray(np.asarray(a, dtype=np.float32))
    norms = np.concatenate([_pk(inputs["ffn1_norm"][0]), _pk(inputs["mix_norm"][0]),
                            _pk(inputs["ffn2_norm"][0]), _pk(inputs["final_norm"])], axis=1)
    shared = {
        "norms": np.ascontiguousarray(norms),
        "bgate": _pk(inputs["b_branch_gate"][0]),
        "sinks": f(inputs["attn_sinks"]).reshape(1, 16),
        "lng": _pk(inputs["sgu_norm_g"][0]),
        "lnb": _pk(inputs["sgu_norm_b"][0]),
        "ws": f(inputs["sgu_w_s"][0]),
        "bs": f(inputs["sgu_b_s"][0]),
        "ident": ident, "negdist": negdist, "tril": tril,
        "f1g": f(inputs["ffn1_w_gate"][0]), "f1u": f(inputs["ffn1_w_up"][0]), "f1d": f(inputs["ffn1_w_down"][0]),
        "win": f(inputs["w_in"][0]), "wpa": f(inputs["w_proj_attn"][0]), "wpb": f(inputs["w_proj_sgu"][0]),
        "wbg": f(inputs["w_branch_gate"][0]), "wout": f(inputs["w_out"][0]),
        "f2g": f(inputs["ffn2_w_gate"][0]), "f2u": f(inputs["ffn2_w_up"][0]), "f2d": f(inputs["ffn2_w_down"][0]),
    }
    x = np.asarray(inputs["x"], dtype=np.float32)
    maps = []
    for c in range(NCORES):
        m = dict(shared)
        m["x"] = np.ascontiguousarray(x[c])
        maps.append(m)
    return maps


def kernel(**inputs):
    key = "full"
    if key not in _NC_CACHE:
        _NC_CACHE[key] = build_nc(stages=("ffn1", "mix", "ffn2", "final"))
    nc = _NC_CACHE[key]
    in_maps = make_in_maps(inputs)
    res = run_bass_kernel_spmd(nc, in_maps, core_ids=list(range(NCORES)))
    out = np.stack([np.asarray(r["y"], dtype=np.float32) for r in res.results], axis=0)
    return out
```

```python
import math
from contextlib import ExitStack

import numpy as np
import concourse.bass as bass
import concourse.mybir as mybir
from concourse.bass_utils import run_bass_kernel_spmd

F32 = mybir.dt.float32
BF16 = mybir.dt.bfloat16
AF = mybir.ActivationFunctionType
ALU = mybir.AluOpType
AX = mybir.AxisListType

D = 2048
S = 2048
DFF = 5632
NCORES = 8
KD = D // 128
KF = DFF // 128
EPS = 1e-6
T = 512
NQ = T // 128
NT = S // T
NSLOT = 7
SLOPES = [2.0 ** (-8.0 * (i + 1) / 16) for i in range(16)]

ENGS = ("pe", "act", "dve", "pool", "sp")


class Op:
    __slots__ = ("eng", "fn", "deps", "signal", "count", "dma_key", "idx")

    def __init__(self, eng, fn, dma_key=None):
        self.eng = eng
        self.fn = fn
        self.deps = []
        self.signal = False
        self.count = 0
        self.dma_key = dma_key
        self.idx = -1


class Sched:
    def __init__(self, nc):
        self.nc = nc
        self.n = 0
        self.by_eng = {e: [] for e in ENGS}
        self.last_writer = {}
        self.readers = {}

    def op(self, eng, fn, reads=(), writes=(), dma_key=None):
        o = Op(eng, fn, dma_key)
        o.idx = self.n
        self.n += 1
        deps = {}
        for r in reads:
            w = self.last_writer.get(r)
            if w is not None:
                deps[w.idx] = (w, True)
        for r in writes:
            w = self.last_writer.get(r)
            if w is not None and w.idx not in deps:
                deps[w.idx] = (w, False)
            rd = self.readers.get(r)
            if rd:
                for x in rd.values():
                    if x.idx not in deps:
                        deps[x.idx] = (x, False)
        for d, raw in deps.values():
            if d.dma_key is None and dma_key is None and d.eng == eng:
                if eng == "pe" or not raw:
                    continue
            o.deps.append(d)
            d.signal = True
        for r in reads:
            self.readers.setdefault(r, {})[eng if dma_key is None else ("dma", o.idx)] = o
        for r in writes:
            self.last_writer[r] = o
            self.readers[r] = {}
        self.by_eng[eng].append(o)
        return o

    def emit(self, final_waits=()):
        nc = self.nc
        dma_keys = []
        dma_cnt = {}
        for e in ENGS:
            c = 0
            for o in self.by_eng[e]:
                if o.dma_key is not None:
                    if o.dma_key not in dma_cnt:
                        dma_cnt[o.dma_key] = 0
                        dma_keys.append(o.dma_key)
                    dma_cnt[o.dma_key] += 16
                    o.count = dma_cnt[o.dma_key]
                elif o.signal:
                    c += 1
                    o.count = c
        with ExitStack() as es:
            sems = {e: es.enter_context(nc.semaphore("s_" + e)) for e in ENGS}
            dsems = {k: es.enter_context(nc.semaphore("d_%d" % i)) for i, k in enumerate(dma_keys)}
            block = es.enter_context(nc.Block())
            engobj = {"pe": "tensor", "act": "scalar", "dve": "vector", "pool": "gpsimd", "sp": "sync"}

            def make(e):
                def body(eng):
                    seen = {}
                    ops = self.by_eng[e]
                    needs = []
                    for o in ops:
                        need = {}
                        for d in o.deps:
                            key = ("d", d.dma_key) if d.dma_key is not None else ("e", d.eng)
                            if d.count > seen.get(key, 0) and d.count > need.get(key, 0):
                                need[key] = d.count
                        for key, cnt in need.items():
                            seen[key] = cnt
                        needs.append(need)
                    if e == "pe":
                        H = 24
                        for idx in range(len(ops)):
                            nd = needs[idx]
                            for key in list(nd):
                                if key[0] == "d" and isinstance(key[1], tuple) and key[1][0] == "slot" and idx >= H:
                                    cnt = nd.pop(key)
                                    tgt = needs[idx - H]
                                    tgt[key] = max(tgt.get(key, 0), cnt)
                    for o, need in zip(ops, needs):
                        for key, cnt in need.items():
                            s = dsems[key[1]] if key[0] == "d" else sems[key[1]]
                            eng.wait_ge(s, cnt)
                        ins = o.fn(eng)
                        if o.dma_key is not None:
                            ins.then_inc(dsems[o.dma_key], 16)
                        elif o.signal:
                            ins.then_inc(sems[e], 1)
                    if e == "sp":
                        done = set()
                        for o in final_waits:
                            if o.dma_key not in done:
                                done.add(o.dma_key)
                                eng.wait_ge(dsems[o.dma_key], dma_cnt[o.dma_key])
                return body

            for e in ENGS:
                if self.by_eng[e] or e == "sp":
                    getattr(block, engobj[e])(make(e))


def build_nc(stages=("ffn1", "mix", "ffn2"), ntiles=NT):
    nc = bass.Bass("TRN2", target_bir_lowering=False)
    Sx = Sched(nc)

    def din(name, shape):
        return nc.dram_tensor(name, list(shape), F32, kind="ExternalInput").ap()

    x_d = din("x", (S, D))
    y_d = nc.dram_tensor("y", [S, D], F32, kind="ExternalOutput").ap()
    norms_d = din("norms", (128, 4 * KD))
    bgate_d = din("bgate", (128, 32))
    sinks_d = din("sinks", (1, 16))
    lng_d = din("lng", (128, 8))
    lnb_d = din("lnb", (128, 8))
    ws_d = din("ws", (16, 128, 128))
    bs_d = din("bs", (16, 128))
    ident_d = din("ident", (128, 128))
    negdist_d = din("negdist", (128, 256))
    tril_d = din("tril", (128, 128))
    W = {}
    for nm, shp in (("f1g", (D, DFF)), ("f1u", (D, DFF)), ("f1d", (DFF, D)),
                    ("win", (D, 3584)), ("wpa", (1024, D)), ("wpb", (1024, D)),
                    ("wbg", (D, 4096)), ("wout", (D, D)),
                    ("f2g", (D, DFF)), ("f2u", (D, DFF)), ("f2d", (DFF, D))):
        W[nm] = din(nm, shp).rearrange("(k p) f -> p k f", p=128)

    xT = nc.alloc_sbuf_tensor("xT", [128, KD, T], F32)
    hT = nc.alloc_sbuf_tensor("hT", [128, KD, T], BF16)
    ring = [nc.alloc_sbuf_tensor("ring%d" % i, [128, 16, 256], BF16) for i in range(NSLOT)]
    REG_F32 = 11264
    reg = nc.alloc_sbuf_tensor("reg", [128, REG_F32], F32)
    regb = reg[:, :].bitcast(BF16)
    hid = regb.rearrange("p (f t) -> p f t", t=T)
    qz = regb[:, 0:16 * T].rearrange("p (c t) -> p c t", t=T)
    zl = regb[:, 8 * T:16 * T].rearrange("p (q c) -> p q c", c=1024)
    guT = regb[:, 16 * T:24 * T].rearrange("p (c t) -> p c t", t=T)
    ga_f = reg[:, 12 * T:12 * T + NQ * 1024]
    gz = ga_f.rearrange("p (q c) -> p q c", c=1024)
    mrg = regb[:, 24 * T:40 * T].rearrange("p (c t) -> p c t", t=T)
    stg = [reg[:, i * 2048:(i + 1) * 2048] for i in range(5)]
    NT4, NT2 = 4, 10
    tmp4 = [nc.alloc_sbuf_tensor("t4_%d" % i, [128, 1024], F32) for i in range(NT4)]
    tmp2 = [nc.alloc_sbuf_tensor("t2_%d" % i, [128, 512], F32) for i in range(NT2)]
    cnt = {"t4": 0, "t2": 0, "ps": 0, "slot": 0, "sm": 0}

    def t4():
        i = cnt["t4"] % NT4
        cnt["t4"] += 1
        return tmp4[i], ("t4", i)

    def t2():
        i = cnt["t2"] % NT2
        cnt["t2"] += 1
        return tmp2[i], ("t2", i)

    NSM = 16
    smalls = nc.alloc_sbuf_tensor("smalls", [128, NSM, 32], F32)

    def sm():
        i = cnt["sm"] % NSM
        cnt["sm"] += 1
        return smalls[:, i, :], ("sm", i)

    lnst = nc.alloc_sbuf_tensor("lnst", [128, 64], F32)
    identf = nc.alloc_sbuf_tensor("identf", [128, 128], F32)
    identb = nc.alloc_sbuf_tensor("identb", [128, 128], BF16)
    onesb = nc.alloc_sbuf_tensor("onesb", [128, 128], BF16)
    negdist = nc.alloc_sbuf_tensor("negdist_sb", [128, 256], F32)
    tril = nc.alloc_sbuf_tensor("tril_sb", [128, 128], F32)
    wsT = nc.alloc_sbuf_tensor("wsT", [128, 16, 128], BF16)
    bsb = nc.alloc_sbuf_tensor("bsb", [128, 8, 128], F32)
    norms = nc.alloc_sbuf_tensor("norms_sb", [128, 4 * KD], F32)
    bgate = nc.alloc_sbuf_tensor("bgate_sb", [128, 32], F32)
    sinkb = nc.alloc_sbuf_tensor("sinkb", [128, 16], F32)
    lngb = nc.alloc_sbuf_tensor("lng_sb", [128, 8], F32)
    lnbb = nc.alloc_sbuf_tensor("lnb_sb", [128, 8], F32)
    kT2 = nc.alloc_sbuf_tensor("kT2", [128, 4, 128 + T], BF16)
    vbuf = nc.alloc_sbuf_tensor("vbuf", [128, 1 + NQ, 256], BF16)

    psum = [nc.alloc_psum_tensor("ps%d" % i, [128, 512], F32) for i in range(8)]

    def bank():
        i = cnt["ps"] % 8
        cnt["ps"] += 1
        return psum[i], ("ps", i)

    op = Sx.op

    def load_small(dst, src, key):
        op("sp", lambda e: e.dma_start(out=dst, in_=src), writes=[key], dma_key=("c", key))

    load_small(identf[:], ident_d[:, :], "identf")
    load_small(norms[:], norms_d[:, :], "norms")
    op("dve", lambda e: e.tensor_copy(out=identb[:], in_=identf[:]), reads=["identf"], writes=["identb"])
    op("dve", lambda e: e.memset(onesb[:], 1.0), writes=["onesb"])
    cst = nc.alloc_sbuf_tensor("cst", [128, 8], F32)
    op("dve", lambda e: e.memset(cst[:], 1.0), writes=["cst"])

    def preload_ln():
        op("act", lambda e: e.activation(out=cst[:, 4:8], in_=cst[:, 0:4], func=AF.Ln), reads=["cst"], writes=["dmy"])

    def setup_mixer_consts():
        load_small(negdist[:], negdist_d[:, :], "negdist")
        load_small(tril[:], tril_d[:, :], "tril")
        load_small(bgate[:], bgate_d[:, :], "bgate")
        load_small(sinkb[:], sinks_d[0:1, :].partition_broadcast(128), "sinkb")
        load_small(lngb[:], lng_d[:, :], "lngb")
        load_small(lnbb[:], lnb_d[:, :], "lnbb")
        for g in range(16):
            op("sp", lambda e, g=g: e.dma_start(out=bsb[(g % 2) * 64:(g % 2 + 1) * 64, g // 2, :],
                                                 in_=bs_d[g:g + 1, :].partition_broadcast(64)),
               writes=["bsb"], dma_key=("c", "bsb"))
        op("dve", lambda e: e.memset(kT2[:], 0.0), writes=[("kT2", c) for c in range(4)] + ["kT2pre"])
        op("dve", lambda e: e.memset(vbuf[:], 0.0), writes=[("vb", q) for q in range(NQ + 1)])
        for half in range(4):
            buf, bk = t4()
            b3 = buf[:, :].rearrange("p (g s) -> p g s", s=128)
            op("sp", lambda e, half=half, b3=b3: e.dma_start(
                out=b3[:, 0:4, :], in_=ws_d[half * 4:(half + 1) * 4, :, :].rearrange("g t s -> t g s")),
               writes=[bk], dma_key=("c", "ws%d" % half))
            mk_buf, mk = t2()
            m3 = mk_buf[:, :].bitcast(BF16)[:, 0:512].rearrange("p (g s) -> p g s", s=128)
            op("dve", lambda e, b3=b3, m3=m3: e.tensor_tensor(
                out=m3, in0=b3[:, 0:4, :], in1=tril[:, :].unsqueeze(1).to_broadcast([128, 4, 128]), op=ALU.mult),
               reads=[bk, "tril"], writes=[mk])
            pb, pk = bank()
            pbb = pb[:, :].bitcast(BF16)
            for gi in range(4):
                op("pe", lambda e, gi=gi, m3=m3, pbb=pbb: e.transpose(
                    out=pbb[:, gi * 128:(gi + 1) * 128], in_=m3[:, gi, :], identity=identb[:]),
                   reads=[mk, "identb"], writes=[pk])
            op("dve", lambda e, half=half, pbb=pbb: e.tensor_copy(
                out=wsT[:, half * 4:(half + 1) * 4, :].rearrange("p g t -> p (g t)"), in_=pbb[:, 0:512]),
               reads=[pk], writes=["wsT"])
        for j in range(8):
            pb, pk = bank()
            for gi in range(2):
                g = 2 * j + gi
                op("pe", lambda e, pb=pb, gi=gi, g=g: e.matmul(pb[gi * 64:(gi + 1) * 64, 0:128], lhsT=onesb[:, 0:64], rhs=wsT[:, g, :],
                                                              start=True, stop=True),
                   reads=["onesb", "wsT"], writes=[pk])
            op("dve", lambda e, pb=pb, j=j: e.scalar_tensor_tensor(
                out=bsb[:, j, :], in0=pb[:, 0:128], scalar=lnbb[:, j:j + 1], in1=bsb[:, j, :], op0=ALU.mult, op1=ALU.add),
               reads=[pk, "bsb", "lnbb"], writes=["bsb2"])

    def wload(wname, k0, kc, c0, ncols=256, dup=None):
        i = cnt["slot"] % NSLOT
        cnt["slot"] += 1
        slot = ring[i]
        key = ("slot", i)
        wv = W[wname]
        if dup is None:
            op("pool", lambda e: e.dma_start(out=slot[:, 0:kc, 0:ncols], in_=wv[:, k0:k0 + kc, c0:c0 + ncols]),
               writes=[key], dma_key=key)
        else:
            for g in range(2):
                for dd in range(2):
                    dst = slot[:, 0:kc, g * 128 + dd * 64:g * 128 + dd * 64 + 64]
                    src = wv[:, k0:k0 + kc, c0 + g * 64:c0 + g * 64 + 64]
                    op("pool", lambda e, dst=dst, src=src: e.dma_start(out=dst, in_=src), writes=[key], dma_key=key)
        return slot, key

    def mm(ps_ap, lhsT, rhs, start, stop, reads, pk):
        op("pe", lambda e: e.matmul(ps_ap, lhsT=lhsT, rhs=rhs, start=start, stop=stop), reads=reads, writes=[pk])

    def xkeys():
        return [("x", k) for k in range(KD)]

    def rstd_from(src_ap, src_keys, n, mul, dst_ap, dst_key, newton=False):
        a_, ak = t2()
        b_, bk = t2()
        a = a_[:, 0:n]
        b = b_[:, 0:n]
        op("dve", lambda e: e.tensor_scalar(out=a, in0=src_ap, scalar1=mul, scalar2=EPS, op0=ALU.mult, op1=ALU.add),
           reads=src_keys, writes=[ak])
        op("act", lambda e: e.activation(out=b, in_=a, func=AF.Ln), reads=[ak], writes=[bk])
        op("act", lambda e: e.activation(out=b, in_=b, func=AF.Exp, scale=-0.5), reads=[bk], writes=[bk])
        if not newton:
            op("dve", lambda e: e.tensor_copy(out=dst_ap, in_=b), reads=[bk], writes=[dst_key])
            return
        c_, ck = t2()
        c = c_[:, 0:n]
        op("dve", lambda e: e.tensor_tensor(out=c, in0=b, in1=b, op=ALU.mult), reads=[bk], writes=[ck])
        op("dve", lambda e: e.tensor_tensor(out=c, in0=c, in1=a, op=ALU.mult), reads=[ck, ak], writes=[ck])
        op("dve", lambda e: e.tensor_scalar(out=c, in0=c, scalar1=-0.5, scalar2=1.5, op0=ALU.mult, op1=ALU.add),
           reads=[ck], writes=[ck])
        op("dve", lambda e: e.tensor_tensor(out=dst_ap, in0=b, in1=c, op=ALU.mult), reads=[bk, ck], writes=[dst_key])

    def rmsnorm(which, final=False):
        pb, pk = bank()
        for k in range(KD):
            sq_, sk = t2()
            sq = sq_[:, :].bitcast(BF16)[:, 0:T]
            op("act", lambda e, k=k, sq=sq: e.activation(out=sq, in_=xT[:, k, :], func=AF.Square),
               reads=[("x", k)], writes=[sk])
            mm(pb[:, 0:T], onesb[:, :], sq, k == 0, k == KD - 1, [sk, "onesb"], pk)
        r_, rk = t2()
        rstd = r_[:, 0:T]
        rstd_from(pb[:, 0:T], [pk], T, 1.0 / D, rstd, rk)
        for k in range(KD):
            if final:
                op("dve", lambda e, k=k: e.scalar_tensor_tensor(
                    out=xT[:, k, :], in0=xT[:, k, :], scalar=norms[:, which * KD + k:which * KD + k + 1],
                    in1=rstd, op0=ALU.mult, op1=ALU.mult),
                   reads=[("x", k), rk, "norms"], writes=[("x", k)])
            else:
                op("dve", lambda e, k=k: e.scalar_tensor_tensor(
                    out=hT[:, k, :], in0=xT[:, k, :], scalar=norms[:, which * KD + k:which * KD + k + 1],
                    in1=rstd, op0=ALU.mult, op1=ALU.mult),
                   reads=[("x", k), rk, "norms"], writes=[("h", k)])

    def ffn(pfx, which_norm):
        rmsnorm(which_norm)
        for cb in range(KF // 2):
            sg, kg = wload(pfx + "g", 0, KD, cb * 256)
            su, ku = wload(pfx + "u", 0, KD, cb * 256)
            for j in range(2):
                f = cb * 2 + j
                pg, pgk = bank()
                pu, puk = bank()
                for k in range(KD):
                    mm(pg[:, 0:T], sg[:, k, j * 128:(j + 1) * 128], hT[:, k, :], k == 0, k == KD - 1, [kg, ("h", k)], pgk)
                for k in range(KD):
                    mm(pu[:, 0:T], su[:, k, j * 128:(j + 1) * 128], hT[:, k, :], k == 0, k == KD - 1, [ku, ("h", k)], puk)
                s_, sk = t2()
                op("act", lambda e, s_=s_, pg=pg: e.activation(out=s_[:, 0:T], in_=pg[:, 0:T], func=AF.Silu),
                   reads=[pgk], writes=[sk])
                op("dve", lambda e, s_=s_, pu=pu, f=f: e.tensor_tensor(out=hid[:, f, :], in0=s_[:, 0:T], in1=pu[:, 0:T], op=ALU.mult),
                   reads=[sk, puk], writes=[("hid", f)])
        preload_ln()
        kgs = [(0, 16), (16, 16), (32, 12)]
        for dp in range(KD // 2):
            pbs = [bank() for _ in range(2)]
            for gi, (k0, kc) in enumerate(kgs):
                sd, kd = wload(pfx + "d", k0, kc, dp * 256)
                for j in range(2):
                    pb, pk = pbs[j]
                    for fl in range(kc):
                        f = k0 + fl
                        mm(pb[:, 0:T], sd[:, fl, j * 128:(j + 1) * 128], hid[:, f, :], f == 0, f == KF - 1,
                           [kd, ("hid", f)], pk)
            for j in range(2):
                d = dp * 2 + j
                pb, pk = pbs[j]
                op("dve", lambda e, pb=pb, d=d: e.scalar_tensor_tensor(
                    out=xT[:, d, :], in0=pb[:, 0:T], scalar=0.5, in1=xT[:, d, :], op0=ALU.mult, op1=ALU.add),
                   reads=[pk, ("x", d)], writes=[("x", d)])

    def UQ(c):
        return [("uq", c, q) for q in range(NQ)]

    def mixer(tile):
        U = lambda n: ("u", n)
        rmsnorm(1)
        for cb in range(4):
            sw, kw = wload("win", 0, KD, cb * 256)
            for j in range(2):
                c = cb * 2 + j
                pb, pk = bank()
                for k in range(KD):
                    mm(pb[:, 0:T], sw[:, k, j * 128:(j + 1) * 128], hT[:, k, :], k == 0, k == KD - 1, [kw, ("h", k)], pk)
                op("act", lambda e, pb=pb, c=c: e.mul(out=qz[:, c, :], in_=pb[:, 0:T], mul=0.125),
                   reads=[pk], writes=UQ(c))
        for gp in range(2):
            sw, kw = wload("win", 0, KD, 1024 + gp * 128, dup=True)
            for j in range(2):
                g = gp * 2 + j
                pb, pk = bank()
                for k in range(KD):
                    mm(pb[:, 0:T], sw[:, k, j * 128:(j + 1) * 128], hT[:, k, :], k == 0, k == KD - 1, [kw, ("h", k)], pk)
                op("dve", lambda e, pb=pb, g=g: e.tensor_copy(out=kT2[:, g, 128:128 + T], in_=pb[:, 0:T]),
                   reads=[pk], writes=[("kT2", g)])
        sw, kw = wload("win", 0, KD, 1280)
        for q in range(NQ):
            pb, pk = bank()
            for k in range(KD):
                mm(pb[:, 0:256], hT[:, k, q * 128:(q + 1) * 128], sw[:, k, :], k == 0, k == KD - 1, [kw, ("h", k)], pk)
            op("act", lambda e, pb=pb, q=q: e.copy(out=vbuf[:, 1 + q, :], in_=pb[:, 0:256]),
               reads=[pk], writes=[("vb", 1 + q)])

        fillers = []
        ustate = {}

        def u_unit(c):
            def f():
                cb, j = c // 2, c % 2
                if j == 0:
                    ustate["u"] = wload("win", 0, KD, 1536 + cb * 256)
                sw, kw = ustate["u"]
                pb, pk = bank()
                for k in range(KD):
                    mm(pb[:, 0:T], sw[:, k, j * 128:(j + 1) * 128], hT[:, k, :], k == 0, k == KD - 1, [kw, ("h", k)], pk)
                op("act", lambda e: e.activation(out=guT[:, c, :], in_=pb[:, 0:T], func=AF.Gelu_apprx_tanh),
                   reads=[pk], writes=[U(16 + c)])
            return f

        def z_unit(zs, q):
            def f():
                if q == 0:
                    ustate["z"] = wload("win", 0, KD, 2560 + zs * 256)
                sw, kw = ustate["z"]
                pb, pk = bank()
                for k in range(KD):
                    mm(pb[:, 0:256], hT[:, k, q * 128:(q + 1) * 128], sw[:, k, :], k == 0, k == KD - 1, [kw, ("h", k)], pk)
                op("act", lambda e: e.activation(out=gz[:, q, zs * 256:(zs + 1) * 256], in_=pb[:, 0:256], func=AF.Gelu_apprx_tanh),
                   reads=[pk], writes=[U(24 + 4 * q + zs)])
            return f

        def ln_stats():
            st_, stk = lnst[:, 0:32], "lnst"
            ustate["st"] = (st_, stk)
            for q in range(NQ):
                bn_, bnk = sm()
                for hh in range(2):
                    op("dve", lambda e, q=q, hh=hh, bn_=bn_: e.bn_stats(out=bn_[:, hh * 6:(hh + 1) * 6], in_=gz[:, q, hh * 512:(hh + 1) * 512]),
                       reads=[U(24 + 4 * q + 2 * hh), U(24 + 4 * q + 2 * hh + 1)], writes=[bnk])
                op("dve", lambda e, q=q, bn_=bn_: e.bn_aggr(out=st_[:, 2 * q:2 * q + 2], in_=bn_[:, 0:12].rearrange("p (a b) -> p a b", a=2)),
                   reads=[bnk], writes=[stk])
            var_ap = st_[:, 0:2 * NQ].rearrange("p (q t) -> p q t", t=2)[:, :, 1]
            rs_, rsk = lnst[:, 32:64], "lnrs"
            ustate["rs"] = (rs_, rsk)
            rstd_from(var_ap, [stk], NQ, 1.0, rs_[:, 0:NQ], rsk)

        def ln_unit(q):
            def f():
                st_, stk = ustate["st"]
                rs_, rsk = ustate["rs"]
                gk = [U(24 + 4 * q + z) for z in range(4)]
                op("dve", lambda e: e.tensor_scalar(out=zl[:, q, :], in0=gz[:, q, :], scalar1=st_[:, 2 * q:2 * q + 1],
                                                    scalar2=rs_[:, q:q + 1], op0=ALU.subtract, op1=ALU.mult),
                   reads=gk + [stk, rsk], writes=[U(8 + 2 * q), U(9 + 2 * q)])
            return f

        def sgu_unit(j):
            def f():
                pb, pk = bank()
                p3 = pb[:, 0:T].rearrange("p (q t) -> p q t", t=128)
                for q in range(NQ):
                    for gi in range(2):
                        g = 2 * j + gi
                        mm(p3[gi * 64:(gi + 1) * 64, q, :], zl[:, q, g * 64:(g + 1) * 64], wsT[:, g, :], True, True,
                           [U(8 + 2 * q), U(9 + 2 * q), "wsT"], pk)
                t_, tk = t2()
                t3 = t_[:, 0:T].rearrange("p (q t) -> p q t", t=128)
                op("dve", lambda e: e.scalar_tensor_tensor(
                    out=t3, in0=p3, scalar=lngb[:, j:j + 1], in1=bsb[:, j, :].unsqueeze(1).to_broadcast([128, NQ, 128]),
                    op0=ALU.mult, op1=ALU.add),
                   reads=[pk, "bsb2", "lngb"], writes=[tk])
                op("dve", lambda e: e.tensor_tensor(out=guT[:, j, :], in0=t_[:, 0:T], in1=guT[:, j, :], op=ALU.mult),
                   reads=[tk, U(16 + j)], writes=[U(16 + j)])
            return f

        for c in range(8):
            u_unit(c)()
        for zs in range(4):
            for q in range(NQ):
                fillers.append(z_unit(zs, q))
        fillers.append(ln_stats)
        for q in range(NQ):
            fillers.append(ln_unit(q))
        for j in range(8):
            fillers.append(sgu_unit(j))

        its = [(qb, g) for qb in range(NQ) for g in range(4)]
        ast = {}

        def att_A1(i):
            qb, g = its[i]
            psA, kA = bank()
            psB, kB = bank()
            s_, sk = t4()
            s3 = s_[:, :].rearrange("p (h k) -> p h k", k=256)
            for hh in range(4):
                hq = 4 * g + hh
                c, po = hq // 2, (hq % 2) * 64
                pp, pkk = (psA, kA) if hh % 2 == 0 else (psB, kB)
                col = (hh // 2) * 256
                mm(pp[:, col:col + 256], qz[po:po + 64, c, qb * 128:(qb + 1) * 128],
                   kT2[po:po + 64, g, qb * 128:qb * 128 + 256], True, True,
                   [("uq", c, qb), ("kT2", g), "kT2pre"], pkk)
            for hh in range(4):
                hq = 4 * g + hh
                pp, pkk = (psA, kA) if hh % 2 == 0 else (psB, kB)
                col = (hh // 2) * 256
                op("dve", lambda e, hh=hh, hq=hq, pp=pp, col=col: e.scalar_tensor_tensor(
                    out=s3[:, hh, :], in0=negdist[:, :], scalar=SLOPES[hq], in1=pp[:, col:col + 256],
                    op0=ALU.mult, op1=ALU.add),
                   reads=[pkk, "negdist"], writes=[sk])
            if tile == 0 and qb == 0:
                op("dve", lambda e: e.memset(s3[:, :, 0:128], -1e30), writes=[sk])
            m_, mk = sm()
            r_, rk = sm()
            op("dve", lambda e: e.tensor_reduce(out=m_[:, 0:4], in_=s3, axis=AX.X, op=ALU.max),
               reads=[sk], writes=[mk])
            op("dve", lambda e: e.tensor_tensor(out=m_[:, 0:4], in0=m_[:, 0:4], in1=sinkb[:, 4 * g:4 * g + 4], op=ALU.max),
               reads=[mk, "sinkb"], writes=[mk])
            op("dve", lambda e: e.tensor_scalar(out=m_[:, 4:8], in0=m_[:, 0:4], scalar1=-1.0, scalar2=None, op0=ALU.mult),
               reads=[mk], writes=[mk])
            op("dve", lambda e: e.tensor_tensor(out=r_[:, 4:8], in0=sinkb[:, 4 * g:4 * g + 4], in1=m_[:, 4:8], op=ALU.add),
               reads=[mk, "sinkb"], writes=[rk])
            op("dve", lambda e: e.memset(r_[:, 0:4], 0.0), writes=[rk])
            ast[i] = {"s3": s3, "sk": sk, "m": m_, "mk": mk, "r": r_, "rk": rk}

        def att_A2(i):
            st = ast[i]
            s3, sk, m_, mk, r_, rk = st["s3"], st["sk"], st["m"], st["mk"], st["r"], st["rk"]
            for hh in range(4):
                op("act", lambda e, hh=hh: e.activation(
                    out=s3[:, hh, :], in_=s3[:, hh, :], func=AF.Exp, bias=m_[:, 4 + hh:5 + hh], scale=1.0,
                    accum_out=r_[:, hh:hh + 1]),
                   reads=[sk, mk, rk], writes=[sk, rk])
            op("act", lambda e: e.activation(out=r_[:, 4:8], in_=r_[:, 4:8], func=AF.Exp),
               reads=[rk], writes=[rk])
            op("dve", lambda e: e.tensor_tensor(out=r_[:, 8:12], in0=r_[:, 0:4], in1=r_[:, 4:8], op=ALU.add),
               reads=[rk], writes=[rk])
            op("dve", lambda e: e.reciprocal(out=r_[:, 12:16], in_=r_[:, 8:12]), reads=[rk], writes=[rk])

        def att_A3(i):
            st = ast[i]
            s3, sk, r_, rk = st["s3"], st["sk"], st["r"], st["rk"]
            p_, pk_ = t2()
            P3 = p_[:, :].bitcast(BF16).rearrange("p (h k) -> p h k", k=256)
            for hh in range(4):
                op("act", lambda e, hh=hh: e.mul(out=P3[:, hh, :], in_=s3[:, hh, :], mul=r_[:, 12 + hh:13 + hh]),
                   reads=[sk, rk], writes=[pk_])
            st["P3"], st["pk"] = P3, pk_

        def att_B1(i):
            P3, pk_ = ast[i]["P3"], ast[i]["pk"]
            pt, ptk = bank()
            ptb = pt[:, :].bitcast(BF16).rearrange("p (kb h q) -> p kb h q", kb=2, h=4)
            for kb in range(2):
                for hh in range(4):
                    op("pe", lambda e, kb=kb, hh=hh: e.transpose(
                        out=ptb[:, kb, hh, :], in_=P3[:, hh, kb * 128:(kb + 1) * 128], identity=identb[:]),
                       reads=[pk_, "identb"], writes=[ptk])
            pts_, ptsk = t2()
            ptsb = pts_[:, :].bitcast(BF16)
            op("dve", lambda e: e.tensor_copy(out=ptsb, in_=pt[:, :].bitcast(BF16)), reads=[ptk], writes=[ptsk])
            ast[i]["pts4"] = ptsb.rearrange("p (kb h q) -> p kb h q", kb=2, h=4)
            ast[i]["ptsk"] = ptsk

        def att_B2(i):
            qb, g = its[i]
            pts4, ptsk = ast[i]["pts4"], ast[i]["ptsk"]
            po_, pok = bank()
            o3 = po_[:, 0:256].rearrange("p (hp q) -> p hp q", q=128)
            for hh in range(4):
                hp, half = hh // 2, hh % 2
                for kb in range(2):
                    mm(o3[half * 64:(half + 1) * 64, hp, :], vbuf[:, qb + kb, g * 64:(g + 1) * 64],
                       pts4[:, kb, hh, :], kb == 0, kb == 1, [("vb", qb + kb), ptsk], pok)
            op("act", lambda e: e.copy(out=qz[:, 2 * g:2 * g + 2, qb * 128:(qb + 1) * 128], in_=o3),
               reads=[pok], writes=[("uq", 2 * g, qb), ("uq", 2 * g + 1, qb)])
            del ast[i]

        nit = len(its)
        stages_att = [att_A1, att_A2, att_A3, att_B1, att_B2]
        nsteps = nit + len(stages_att) - 1
        fi = 0
        for step in range(nsteps):
            for lag, fn in enumerate(stages_att):
                i = step - lag
                if 0 <= i < nit:
                    fn(i)
            nz = 4 * NQ + 1
            if step < 4:
                want = (nz * (step + 1) + 3) // 4
            else:
                want = nz + ((len(fillers) - nz) * (step - 3) + (nsteps - 8) - 1) // max(1, nsteps - 8)
            while fi < min(want, len(fillers)):
                fillers[fi]()
                fi += 1
        while fi < len(fillers):
            fillers[fi]()
            fi += 1
        op("dve", lambda e: e.tensor_copy(out=kT2[:, :, 0:128], in_=kT2[:, :, T:T + 128]),
           reads=[("kT2", g) for g in range(4)], writes=["kT2pre"])
        op("dve", lambda e: e.tensor_copy(out=vbuf[:, 0, :], in_=vbuf[:, NQ, :]), reads=[("vb", NQ)], writes=[("vb", 0)])
        for dp in range(8):
            i = cnt["slot"] % NSLOT
            cnt["slot"] += 1
            sab, kab = ring[i], ("slot", i)
            op("pool", lambda e, sab=sab, dp=dp: e.dma_start(out=sab[:, 0:8, :], in_=W["wpa"][:, 0:8, dp * 256:(dp + 1) * 256]),
               writes=[kab], dma_key=kab)
            op("pool", lambda e, sab=sab, dp=dp: e.dma_start(out=sab[:, 8:16, :], in_=W["wpb"][:, 0:8, dp * 256:(dp + 1) * 256]),
               writes=[kab], dma_key=kab)
            sga, kga = wload("wbg", 0, KD, dp * 256)
            sgb, kgb = wload("wbg", 0, KD, 2048 + dp * 256)
            for j in range(2):
                d = dp * 2 + j
                cs = slice(j * 128, (j + 1) * 128)
                pGA, pGAk = bank()
                pGB, pGBk = bank()
                pA, pAk = bank()
                pB, pBk = bank()

                def g_groups():
                    for k in range(KD):
                        mm(pGA[:, 0:T], sga[:, k, cs], hT[:, k, :], k == 0, k == KD - 1, [kga, ("h", k)], pGAk)
                    for k in range(KD):
                        mm(pGB[:, 0:T], sgb[:, k, cs], hT[:, k, :], k == 0, k == KD - 1, [kgb, ("h", k)], pGBk)

                def p_groups():
                    for c in range(8):
                        mm(pA[:, 0:T], sab[:, c, cs], qz[:, c, :], c == 0, c == 7, [kab] + UQ(c), pAk)
                    for c in range(8):
                        mm(pB[:, 0:T], sab[:, 8 + c, cs], guT[:, c, :], c == 0, c == 7, [kab, ("u", 16 + c)], pBk)

                if j == 0:
                    g_groups()
                    p_groups()
                else:
                    p_groups()
                    g_groups()
                ga_, gak = t2()
                gb_, gbk = t2()
                op("act", lambda e, ga_=ga_, pGA=pGA, d=d: e.activation(out=ga_[:, 0:T], in_=pGA[:, 0:T], func=AF.Sigmoid, bias=bgate[:, d:d + 1], scale=1.0),
                   reads=[pGAk, "bgate"], writes=[gak])
                op("act", lambda e, gb_=gb_, pGB=pGB, d=d: e.activation(out=gb_[:, 0:T], in_=pGB[:, 0:T], func=AF.Sigmoid, bias=bgate[:, 16 + d:17 + d], scale=1.0),
                   reads=[pGBk, "bgate"], writes=[gbk])
                op("dve", lambda e, ga_=ga_, pA=pA: e.tensor_tensor(out=ga_[:, 0:T], in0=ga_[:, 0:T], in1=pA[:, 0:T], op=ALU.mult),
                   reads=[gak, pAk], writes=[gak])
                op("dve", lambda e, gb_=gb_, pB=pB: e.tensor_tensor(out=gb_[:, 0:T], in0=gb_[:, 0:T], in1=pB[:, 0:T], op=ALU.mult),
                   reads=[gbk, pBk], writes=[gbk])
                op("dve", lambda e, ga_=ga_, gb_=gb_, d=d: e.tensor_tensor(out=mrg[:, d, :], in0=ga_[:, 0:T], in1=gb_[:, 0:T], op=ALU.add),
                   reads=[gak, gbk], writes=[("u", 24 + d)])
        preload_ln()
        for dp in range(8):
            so, ko = wload("wout", 0, KD, dp * 256)
            for j in range(2):
                d = dp * 2 + j
                pb, pk = bank()
                for k in range(KD):
                    mm(pb[:, 0:T], so[:, k, j * 128:(j + 1) * 128], mrg[:, k, :], k == 0, k == KD - 1, [ko, ("u", 24 + k)], pk)
                op("dve", lambda e, pb=pb, d=d: e.tensor_tensor(out=xT[:, d, :], in0=pb[:, 0:T], in1=xT[:, d, :], op=ALU.add),
                   reads=[pk, ("x", d)], writes=[("x", d)])

    out_ops = []
    final = "final" in stages
    cin = {"n": 0}

    def stg_keys(i):
        return [("stg", i)] + [("hid", f) for f in range(8 * i, 8 * i + 8)]

    def x_load(t0, q):
        i = 2 + cin["n"] % 3
        cin["n"] += 1
        sg_ = stg[i]
        op("sp", lambda e: e.dma_start(out=sg_, in_=x_d[t0 + q * 128:t0 + (q + 1) * 128, :]),
           writes=stg_keys(i), dma_key=("stg", i))
        return i

    def x_in(q, i):
        sg_ = stg[i]
        for kk in range(KD // 4):
            pb, pk = bank()
            for j in range(4):
                k = kk * 4 + j
                op("pe", lambda e, pb=pb, j=j, k=k: e.transpose(
                    out=pb[:, j * 128:(j + 1) * 128], in_=sg_[:, k * 128:(k + 1) * 128], identity=identf[:]),
                   reads=[("stg", i), "identf"], writes=[pk])
            wk = [("x", kk * 4 + j) for j in range(4)] + [("xq", kk * 4 + j, q) for j in range(4)]
            if kk % 2 == 0:
                op("act", lambda e, pb=pb, kk=kk: e.copy(
                    out=xT[:, kk * 4:(kk + 1) * 4, q * 128:(q + 1) * 128], in_=pb[:, :].rearrange("p (i t) -> p i t", t=128)),
                   reads=[pk], writes=wk)
            else:
                op("dve", lambda e, pb=pb, kk=kk: e.tensor_copy(
                    out=xT[:, kk * 4:(kk + 1) * 4, q * 128:(q + 1) * 128], in_=pb[:, :].rearrange("p (i t) -> p i t", t=128)),
                   reads=[pk], writes=wk)

    def y_out(t0, q, rt):
        io = q % 2
        sg_ = stg[io]
        for kk in range(KD // 4):
            pb, pk = bank()
            for j in range(4):
                k = kk * 4 + j
                op("pe", lambda e, pb=pb, j=j, k=k: e.transpose(
                    out=pb[:, j * 128:(j + 1) * 128], in_=xT[:, k, q * 128:(q + 1) * 128], identity=identf[:]),
                   reads=[("xq", k, q), "identf"], writes=[pk])
            if rt is not None:
                rt_, rtk = rt
                if kk % 2 == 0:
                    op("act", lambda e, pb=pb, kk=kk: e.mul(out=sg_[:, kk * 512:(kk + 1) * 512], in_=pb[:, :], mul=rt_[:, q:q + 1]),
                       reads=[pk, rtk], writes=stg_keys(io))
                else:
                    op("dve", lambda e, pb=pb, kk=kk: e.tensor_scalar(out=sg_[:, kk * 512:(kk + 1) * 512], in0=pb[:, :],
                                                                    scalar1=rt_[:, q:q + 1], scalar2=None, op0=ALU.mult),
                       reads=[pk, rtk], writes=stg_keys(io))
            elif kk % 2 == 0:
                op("act", lambda e, pb=pb, kk=kk: e.copy(out=sg_[:, kk * 512:(kk + 1) * 512], in_=pb[:, :]),
                   reads=[pk], writes=stg_keys(io))
            else:
                op("dve", lambda e, pb=pb, kk=kk: e.tensor_copy(out=sg_[:, kk * 512:(kk + 1) * 512], in_=pb[:, :]),
                   reads=[pk], writes=stg_keys(io))
        o = op("sp", lambda e: e.dma_start(out=y_d[t0 + q * 128:t0 + (q + 1) * 128, :], in_=sg_),
               reads=stg_keys(io), dma_key=("out", io))
        out_ops.append(o)

    pend = [x_load(0, q) for q in range(min(3, NQ))]
    for q in range(NQ):
        if q + 3 < NQ:
            pass
        x_in(q, pend[q])
        if q + 3 < NQ:
            pend.append(x_load(0, q + 3))
    for tile in range(ntiles):
        t0 = tile * T
        if "ffn1" in stages:
            ffn("f1", 0)
        if "mix" in stages:
            if tile == 0:
                setup_mixer_consts()
            mixer(tile)
        if "ffn2" in stages:
            ffn("f2", 2)
        rt = None
        if final:
            pst, pstk = bank()
            for k in range(KD):
                sq_, sk = t2()
                sq = sq_[:, :].bitcast(BF16)[:, 0:T]
                op("act", lambda e, k=k, sq=sq: e.activation(out=sq, in_=xT[:, k, :], func=AF.Square),
                   reads=[("x", k)], writes=[sk])
                for q in range(NQ):
                    op("pe", lambda e, k=k, q=q, sq=sq, pst=pst: e.matmul(
                        pst[:, q:q + 1], lhsT=sq[:, q * 128:(q + 1) * 128], rhs=onesb[:, 0:1],
                        start=(k == 0 and q == 0), stop=(k == KD - 1), skip_group_check=True),
                       reads=[sk, "onesb"], writes=[pstk])
                op("dve", lambda e, k=k: e.tensor_scalar(out=xT[:, k, :], in0=xT[:, k, :], scalar1=norms[:, 3 * KD + k:3 * KD + k + 1],
                                                         scalar2=None, op0=ALU.mult),
                   reads=[("x", k), "norms"], writes=[("x", k)] + [("xq", k, q) for q in range(NQ)])
            rt_, rtk = sm()
            rstd_from(pst[:, 0:NQ], [pstk], NQ, 1.0 / D, rt_[:, 0:NQ], rtk)
            rt = (rt_, rtk)
        else:
            for k in range(KD):
                op("dve", lambda e, k=k: e.tensor_copy(out=xT[:, k, 0:1], in_=xT[:, k, 0:1]),
                   reads=[("x", k)], writes=[("x", k)] + [("xq", k, q) for q in range(NQ)])
        nxt = tile + 1 < ntiles
        t1 = (tile + 1) * T
        pend = [x_load(t1, q) for q in range(min(3, NQ))] if nxt else []
        for q in range(NQ):
            y_out(t0, q, rt)
            if nxt and q >= 1:
                qi = q - 1
                x_in(qi, pend[qi])
                if qi + 3 < NQ:
                    pend.append(x_load(t1, qi + 3))
        if nxt:
            qi = NQ - 1
            x_in(qi, pend[qi])
    Sx.emit(final_waits=out_ops)
    return nc


_NC_CACHE = {}


def _consts():
    i = np.arange(128, dtype=np.float64)[:, None]
    j = np.arange(256, dtype=np.float64)[None, :]
    dist = 128.0 + i - j
    valid = (dist >= 0) & (dist < 128)
    negdist = np.where(valid, -dist, -1e32).astype(np.float32)
    t = np.arange(128)[:, None]
    s = np.arange(128)[None, :]
    tril = (s <= t).astype(np.float32)
    return np.eye(128, dtype=np.float32), negdist, tril


def _pk(v):
    return np.ascontiguousarray(np.asarray(v, dtype=np.float32).reshape(-1, 128).T)


def make_in_maps(inputs):
    ident, negdist, tril = _consts()
    f = lambda a: np.ascontiguousarray(np.asarray(a, dtype=np.float32))
    norms = np.concatenate([_pk(inputs["ffn1_norm"][0]), _pk(inputs["mix_norm"][0]),
                            _pk(inputs["ffn2_norm"][0]), _pk(inputs["final_norm"])], axis=1)
    shared = {
        "norms": np.ascontiguousarray(norms),
        "bgate": _pk(inputs["b_branch_gate"][0]),
        "sinks": f(inputs["attn_sinks"]).reshape(1, 16),
        "lng": _pk(inputs["sgu_norm_g"][0]),
        "lnb": _pk(inputs["sgu_norm_b"][0]),
        "ws": f(inputs["sgu_w_s"][0]),
        "bs": f(inputs["sgu_b_s"][0]),
        "ident": ident, "negdist": negdist, "tril": tril,
        "f1g": f(inputs["ffn1_w_gate"][0]), "f1u": f(inputs["ffn1_w_up"][0]), "f1d": f(inputs["ffn1_w_down"][0]),
        "win": f(inputs["w_in"][0]), "wpa": f(inputs["w_proj_attn"][0]), "wpb": f(inputs["w_proj_sgu"][0]),
        "wbg": f(inputs["w_branch_gate"][0]), "wout": f(inputs["w_out"][0]),
        "f2g": f(inputs["ffn2_w_gate"][0]), "f2u": f(inputs["ffn2_w_up"][0]), "f2d": f(inputs["ffn2_w_down"][0]),
    }
    x = np.asarray(inputs["x"], dtype=np.float32)
    maps = []
    for c in range(NCORES):
        m = dict(shared)
        m["x"] = np.ascontiguousarray(x[c])
        maps.append(m)
    return maps


def kernel(**inputs):
    key = "full"
    if key not in _NC_CACHE:
        _NC_CACHE[key] = build_nc(stages=("ffn1", "mix", "ffn2", "final"))
    nc = _NC_CACHE[key]
    in_maps = make_in_maps(inputs)
    res = run_bass_kernel_spmd(nc, in_maps, core_ids=list(range(NCORES)))
    out = np.stack([np.asarray(r["y"], dtype=np.float32) for r in res.results], axis=0)
    return out
```

```python
import math
from contextlib import ExitStack

import numpy as np
import concourse.bass as bass
import concourse.mybir as mybir
from concourse.bass_utils import run_bass_kernel_spmd

F32 = mybir.dt.float32
BF16 = mybir.dt.bfloat16
AF = mybir.ActivationFunctionType
ALU = mybir.AluOpType
AX = mybir.AxisListType

D = 2048
S = 2048
DFF = 5632
NCORES = 8
KD = D // 128
KF = DFF // 128
EPS = 1e-6
T = 512
NQ = T // 128
NT = S // T
NSLOT = 7
SLOPES = [2.0 ** (-8.0 * (i + 1) / 16) for i in range(16)]

ENGS = ("pe", "act", "dve", "pool", "sp")


class Op:
    __slots__ = ("eng", "fn", "deps", "signal", "count", "dma_key", "idx")

    def __init__(self, eng, fn, dma_key=None):
        self.eng = eng
        self.fn = fn
        self.deps = []
        self.signal = False
        self.count = 0
        self.dma_key = dma_key
        self.idx = -1


class Sched:
    def __init__(self, nc):
        self.nc = nc
        self.n = 0
        self.by_eng = {e: [] for e in ENGS}
        self.last_writer = {}
        self.readers = {}

    def op(self, eng, fn, reads=(), writes=(), dma_key=None):
        o = Op(eng, fn, dma_key)
        o.idx = self.n
        self.n += 1
        deps = {}
        for r in reads:
            w = self.last_writer.get(r)
            if w is not None:
                deps[w.idx] = (w, True)
        for r in writes:
            w = self.last_writer.get(r)
            if w is not None and w.idx not in deps:
                deps[w.idx] = (w, False)
            rd = self.readers.get(r)
            if rd:
                for x in rd.values():
                    if x.idx not in deps:
                        deps[x.idx] = (x, False)
        for d, raw in deps.values():
            if d.dma_key is None and dma_key is None and d.eng == eng:
                if eng == "pe" or not raw:
                    continue
            o.deps.append(d)
            d.signal = True
        for r in reads:
            self.readers.setdefault(r, {})[eng if dma_key is None else ("dma", o.idx)] = o
        for r in writes:
            self.last_writer[r] = o
            self.readers[r] = {}
        self.by_eng[eng].append(o)
        return o

    def emit(self, final_waits=()):
        nc = self.nc
        dma_keys = []
        dma_cnt = {}
        for e in ENGS:
            c = 0
            for o in self.by_eng[e]:
                if o.dma_key is not None:
                    if o.dma_key not in dma_cnt:
                        dma_cnt[o.dma_key] = 0
                        dma_keys.append(o.dma_key)
                    dma_cnt[o.dma_key] += 16
                    o.count = dma_cnt[o.dma_key]
                elif o.signal:
                    c += 1
                    o.count = c
        with ExitStack() as es:
            sems = {e: es.enter_context(nc.semaphore("s_" + e)) for e in ENGS}
            dsems = {k: es.enter_context(nc.semaphore("d_%d" % i)) for i, k in enumerate(dma_keys)}
            block = es.enter_context(nc.Block())
            engobj = {"pe": "tensor", "act": "scalar", "dve": "vector", "pool": "gpsimd", "sp": "sync"}

            def make(e):
                def body(eng):
                    seen = {}
                    ops = self.by_eng[e]
                    needs = []
                    for o in ops:
                        need = {}
                        for d in o.deps:
                            key = ("d", d.dma_key) if d.dma_key is not None else ("e", d.eng)
                            if d.count > seen.get(key, 0) and d.count > need.get(key, 0):
                                need[key] = d.count
                        for key, cnt in need.items():
                            seen[key] = cnt
                        needs.append(need)
                    if e == "pe":
                        H = 24
                        for idx in range(len(ops)):
                            nd = needs[idx]
                            for key in list(nd):
                                if key[0] == "d" and isinstance(key[1], tuple) and key[1][0] == "slot" and idx >= H:
                                    cnt = nd.pop(key)
                                    tgt = needs[idx - H]
                                    tgt[key] = max(tgt.get(key, 0), cnt)
                    for o, need in zip(ops, needs):
                        for key, cnt in need.items():
                            s = dsems[key[1]] if key[0] == "d" else sems[key[1]]
                            eng.wait_ge(s, cnt)
                        ins = o.fn(eng)
                        if o.dma_key is not None:
                            ins.then_inc(dsems[o.dma_key], 16)
                        elif o.signal:
                            ins.then_inc(sems[e], 1)
                    if e == "sp":
                        done = set()
                        for o in final_waits:
                            if o.dma_key not in done:
                                done.add(o.dma_key)
                                eng.wait_ge(dsems[o.dma_key], dma_cnt[o.dma_key])
                return body

            for e in ENGS:
                if self.by_eng[e] or e == "sp":
                    getattr(block, engobj[e])(make(e))


def build_nc(stages=("ffn1", "mix", "ffn2"), ntiles=NT):
    nc = bass.Bass("TRN2", target_bir_lowering=False)
    Sx = Sched(nc)

    def din(name, shape):
        return nc.dram_tensor(name, list(shape), F32, kind="ExternalInput").ap()

    x_d = din("x", (S, D))
    y_d = nc.dram_tensor("y", [S, D], F32, kind="ExternalOutput").ap()
    norms_d = din("norms", (128, 4 * KD))
    bgate_d = din("bgate", (128, 32))
    sinks_d = din("sinks", (1, 16))
    lng_d = din("lng", (128, 8))
    lnb_d = din("lnb", (128, 8))
    ws_d = din("ws", (16, 128, 128))
    bs_d = din("bs", (16, 128))
    ident_d = din("ident", (128, 128))
    negdist_d = din("negdist", (128, 256))
    tril_d = din("tril", (128, 128))
    W = {}
    for nm, shp in (("f1g", (D, DFF)), ("f1u", (D, DFF)), ("f1d", (DFF, D)),
                    ("win", (D, 3584)), ("wpa", (1024, D)), ("wpb", (1024, D)),
                    ("wbg", (D, 4096)), ("wout", (D, D)),
                    ("f2g", (D, DFF)), ("f2u", (D, DFF)), ("f2d", (DFF, D))):
        W[nm] = din(nm, shp).rearrange("(k p) f -> p k f", p=128)

    xT = nc.alloc_sbuf_tensor("xT", [128, KD, T], F32)
    hT = nc.alloc_sbuf_tensor("hT", [128, KD, T], BF16)
    ring = [nc.alloc_sbuf_tensor("ring%d" % i, [128, 16, 256], BF16) for i in range(NSLOT)]
    REG_F32 = 11264
    reg = nc.alloc_sbuf_tensor("reg", [128, REG_F32], F32)
    regb = reg[:, :].bitcast(BF16)
    hid = regb.rearrange("p (f t) -> p f t", t=T)
    qz = regb[:, 0:16 * T].rearrange("p (c t) -> p c t", t=T)
    zl = regb[:, 8 * T:16 * T].rearrange("p (q c) -> p q c", c=1024)
    guT = regb[:, 16 * T:24 * T].rearrange("p (c t) -> p c t", t=T)
    ga_f = reg[:, 12 * T:12 * T + NQ * 1024]
    gz = ga_f.rearrange("p (q c) -> p q c", c=1024)
    mrg = regb[:, 24 * T:40 * T].rearrange("p (c t) -> p c t", t=T)
    stg = [reg[:, i * 2048:(i + 1) * 2048] for i in range(5)]
    NT4, NT2 = 4, 10
    tmp4 = [nc.alloc_sbuf_tensor("t4_%d" % i, [128, 1024], F32) for i in range(NT4)]
    tmp2 = [nc.alloc_sbuf_tensor("t2_%d" % i, [128, 512], F32) for i in range(NT2)]
    cnt = {"t4": 0, "t2": 0, "ps": 0, "slot": 0, "sm": 0}

    def t4():
        i = cnt["t4"] % NT4
        cnt["t4"] += 1
        return tmp4[i], ("t4", i)

    def t2():
        i = cnt["t2"] % NT2
        cnt["t2"] += 1
        return tmp2[i], ("t2", i)

    NSM = 16
    smalls = nc.alloc_sbuf_tensor("smalls", [128, NSM, 32], F32)

    def sm():
        i = cnt["sm"] % NSM
        cnt["sm"] += 1
        return smalls[:, i, :], ("sm", i)

    lnst = nc.alloc_sbuf_tensor("lnst", [128, 64], F32)
    identf = nc.alloc_sbuf_tensor("identf", [128, 128], F32)
    identb = nc.alloc_sbuf_tensor("identb", [128, 128], BF16)
    onesb = nc.alloc_sbuf_tensor("onesb", [128, 128], BF16)
    negdist = nc.alloc_sbuf_tensor("negdist_sb", [128, 256], F32)
    tril = nc.alloc_sbuf_tensor("tril_sb", [128, 128], F32)
    wsT = nc.alloc_sbuf_tensor("wsT", [128, 16, 128], BF16)
    bsb = nc.alloc_sbuf_tensor("bsb", [128, 8, 128], F32)
    norms = nc.alloc_sbuf_tensor("norms_sb", [128, 4 * KD], F32)
    bgate = nc.alloc_sbuf_tensor("bgate_sb", [128, 32], F32)
    sinkb = nc.alloc_sbuf_tensor("sinkb", [128, 16], F32)
    lngb = nc.alloc_sbuf_tensor("lng_sb", [128, 8], F32)
    lnbb = nc.alloc_sbuf_tensor("lnb_sb", [128, 8], F32)
    kT2 = nc.alloc_sbuf_tensor("kT2", [128, 4, 128 + T], BF16)
    vbuf = nc.alloc_sbuf_tensor("vbuf", [128, 1 + NQ, 256], BF16)

    psum = [nc.alloc_psum_tensor("ps%d" % i, [128, 512], F32) for i in range(8)]

    def bank():
        i = cnt["ps"] % 8
        cnt["ps"] += 1
        return psum[i], ("ps", i)

    op = Sx.op

    def load_small(dst, src, key):
        op("sp", lambda e: e.dma_start(out=dst, in_=src), writes=[key], dma_key=("c", key))

    load_small(identf[:], ident_d[:, :], "identf")
    load_small(norms[:], norms_d[:, :], "norms")
    op("dve", lambda e: e.tensor_copy(out=identb[:], in_=identf[:]), reads=["identf"], writes=["identb"])
    op("dve", lambda e: e.memset(onesb[:], 1.0), writes=["onesb"])
    cst = nc.alloc_sbuf_tensor("cst", [128, 8], F32)
    op("dve", lambda e: e.memset(cst[:], 1.0), writes=["cst"])

    def preload_ln():
        op("act", lambda e: e.activation(out=cst[:, 4:8], in_=cst[:, 0:4], func=AF.Ln), reads=["cst"], writes=["dmy"])

    def setup_mixer_consts():
        load_small(negdist[:], negdist_d[:, :], "negdist")
        load_small(tril[:], tril_d[:, :], "tril")
        load_small(bgate[:], bgate_d[:, :], "bgate")
        load_small(sinkb[:], sinks_d[0:1, :].partition_broadcast(128), "sinkb")
        load_small(lngb[:], lng_d[:, :], "lngb")
        load_small(lnbb[:], lnb_d[:, :], "lnbb")
        for g in range(16):
            op("sp", lambda e, g=g: e.dma_start(out=bsb[(g % 2) * 64:(g % 2 + 1) * 64, g // 2, :],
                                                 in_=bs_d[g:g + 1, :].partition_broadcast(64)),
               writes=["bsb"], dma_key=("c", "bsb"))
        op("dve", lambda e: e.memset(kT2[:], 0.0), writes=[("kT2", c) for c in range(4)] + ["kT2pre"])
        op("dve", lambda e: e.memset(vbuf[:], 0.0), writes=[("vb", q) for q in range(NQ + 1)])
        for half in range(4):
            buf, bk = t4()
            b3 = buf[:, :].rearrange("p (g s) -> p g s", s=128)
            op("sp", lambda e, half=half, b3=b3: e.dma_start(
                out=b3[:, 0:4, :], in_=ws_d[half * 4:(half + 1) * 4, :, :].rearrange("g t s -> t g s")),
               writes=[bk], dma_key=("c", "ws%d" % half))
            mk_buf, mk = t2()
            m3 = mk_buf[:, :].bitcast(BF16)[:, 0:512].rearrange("p (g s) -> p g s", s=128)
            op("dve", lambda e, b3=b3, m3=m3: e.tensor_tensor(
                out=m3, in0=b3[:, 0:4, :], in1=tril[:, :].unsqueeze(1).to_broadcast([128, 4, 128]), op=ALU.mult),
               reads=[bk, "tril"], writes=[mk])
            pb, pk = bank()
            pbb = pb[:, :].bitcast(BF16)
            for gi in range(4):
                op("pe", lambda e, gi=gi, m3=m3, pbb=pbb: e.transpose(
                    out=pbb[:, gi * 128:(gi + 1) * 128], in_=m3[:, gi, :], identity=identb[:]),
                   reads=[mk, "identb"], writes=[pk])
            op("dve", lambda e, half=half, pbb=pbb: e.tensor_copy(
                out=wsT[:, half * 4:(half + 1) * 4, :].rearrange("p g t -> p (g t)"), in_=pbb[:, 0:512]),
               reads=[pk], writes=["wsT"])
        for j in range(8):
            pb, pk = bank()
            for gi in range(2):
                g = 2 * j + gi
                op("pe", lambda e, pb=pb, gi=gi, g=g: e.matmul(pb[gi * 64:(gi + 1) * 64, 0:128], lhsT=onesb[:, 0:64], rhs=wsT[:, g, :],
                                                              start=True, stop=True),
                   reads=["onesb", "wsT"], writes=[pk])
            op("dve", lambda e, pb=pb, j=j: e.scalar_tensor_tensor(
                out=bsb[:, j, :], in0=pb[:, 0:128], scalar=lnbb[:, j:j + 1], in1=bsb[:, j, :], op0=ALU.mult, op1=ALU.add),
               reads=[pk, "bsb", "lnbb"], writes=["bsb2"])

    def wload(wname, k0, kc, c0, ncols=256, dup=None):
        i = cnt["slot"] % NSLOT
        cnt["slot"] += 1
        slot = ring[i]
        key = ("slot", i)
        wv = W[wname]
        if dup is None:
            op("pool", lambda e: e.dma_start(out=slot[:, 0:kc, 0:ncols], in_=wv[:, k0:k0 + kc, c0:c0 + ncols]),
               writes=[key], dma_key=key)
        else:
            for g in range(2):
                for dd in range(2):
                    dst = slot[:, 0:kc, g * 128 + dd * 64:g * 128 + dd * 64 + 64]
                    src = wv[:, k0:k0 + kc, c0 + g * 64:c0 + g * 64 + 64]
                    op("pool", lambda e, dst=dst, src=src: e.dma_start(out=dst, in_=src), writes=[key], dma_key=key)
        return slot, key

    def mm(ps_ap, lhsT, rhs, start, stop, reads, pk):
        op("pe", lambda e: e.matmul(ps_ap, lhsT=lhsT, rhs=rhs, start=start, stop=stop), reads=reads, writes=[pk])

    def xkeys():
        return [("x", k) for k in range(KD)]

    def rstd_from(src_ap, src_keys, n, mul, dst_ap, dst_key, newton=False):
        a_, ak = t2()
        b_, bk = t2()
        a = a_[:, 0:n]
        b = b_[:, 0:n]
        op("dve", lambda e: e.tensor_scalar(out=a, in0=src_ap, scalar1=mul, scalar2=EPS, op0=ALU.mult, op1=ALU.add),
           reads=src_keys, writes=[ak])
        op("act", lambda e: e.activation(out=b, in_=a, func=AF.Ln), reads=[ak], writes=[bk])
        op("act", lambda e: e.activation(out=b, in_=b, func=AF.Exp, scale=-0.5), reads=[bk], writes=[bk])
        if not newton:
            op("dve", lambda e: e.tensor_copy(out=dst_ap, in_=b), reads=[bk], writes=[dst_key])
            return
        c_, ck = t2()
        c = c_[:, 0:n]
        op("dve", lambda e: e.tensor_tensor(out=c, in0=b, in1=b, op=ALU.mult), reads=[bk], writes=[ck])
        op("dve", lambda e: e.tensor_tensor(out=c, in0=c, in1=a, op=ALU.mult), reads=[ck, ak], writes=[ck])
        op("dve", lambda e: e.tensor_scalar(out=c, in0=c, scalar1=-0.5, scalar2=1.5, op0=ALU.mult, op1=ALU.add),
           reads=[ck], writes=[ck])
        op("dve", lambda e: e.tensor_tensor(out=dst_ap, in0=b, in1=c, op=ALU.mult), reads=[bk, ck], writes=[dst_key])

    def rmsnorm(which, final=False):
        pb, pk = bank()
        for k in range(KD):
            sq_, sk = t2()
            sq = sq_[:, :].bitcast(BF16)[:, 0:T]
            op("act", lambda e, k=k, sq=sq: e.activation(out=sq, in_=xT[:, k, :], func=AF.Square),
               reads=[("x", k)], writes=[sk])
            mm(pb[:, 0:T], onesb[:, :], sq, k == 0, k == KD - 1, [sk, "onesb"], pk)
        r_, rk = t2()
        rstd = r_[:, 0:T]
        rstd_from(pb[:, 0:T], [pk], T, 1.0 / D, rstd, rk)
        for k in range(KD):
            if final:
                op("dve", lambda e, k=k: e.scalar_tensor_tensor(
                    out=xT[:, k, :], in0=xT[:, k, :], scalar=norms[:, which * KD + k:which * KD + k + 1],
                    in1=rstd, op0=ALU.mult, op1=ALU.mult),
                   reads=[("x", k), rk, "norms"], writes=[("x", k)])
            else:
                op("dve", lambda e, k=k: e.scalar_tensor_tensor(
                    out=hT[:, k, :], in0=xT[:, k, :], scalar=norms[:, which * KD + k:which * KD + k + 1],
                    in1=rstd, op0=ALU.mult, op1=ALU.mult),
                   reads=[("x", k), rk, "norms"], writes=[("h", k)])

    def ffn(pfx, which_norm):
        rmsnorm(which_norm)
        for cb in range(KF // 2):
            sg, kg = wload(pfx + "g", 0, KD, cb * 256)
            su, ku = wload(pfx + "u", 0, KD, cb * 256)
            banks4 = [(bank(), bank()) for _ in range(2)]
            if cb == 0:
                for k in range(KD):
                    for j in range(2):
                        (pg, pgk), (pu, puk) = banks4[j]
                        mm(pg[:, 0:T], sg[:, k, j * 128:(j + 1) * 128], hT[:, k, :], k == 0, k == KD - 1, [kg, ("h", k)], pgk)
                        mm(pu[:, 0:T], su[:, k, j * 128:(j + 1) * 128], hT[:, k, :], k == 0, k == KD - 1, [ku, ("h", k)], puk)
            for j in range(2):
                f = cb * 2 + j
                (pg, pgk), (pu, puk) = banks4[j]
                if cb != 0:
                    for k in range(KD):
                        mm(pg[:, 0:T], sg[:, k, j * 128:(j + 1) * 128], hT[:, k, :], k == 0, k == KD - 1, [kg, ("h", k)], pgk)
                    for k in range(KD):
                        mm(pu[:, 0:T], su[:, k, j * 128:(j + 1) * 128], hT[:, k, :], k == 0, k == KD - 1, [ku, ("h", k)], puk)
                s_, sk = t2()
                op("act", lambda e, s_=s_, pg=pg: e.activation(out=s_[:, 0:T], in_=pg[:, 0:T], func=AF.Silu),
                   reads=[pgk], writes=[sk])
                op("dve", lambda e, s_=s_, pu=pu, f=f: e.tensor_tensor(out=hid[:, f, :], in0=s_[:, 0:T], in1=pu[:, 0:T], op=ALU.mult),
                   reads=[sk, puk], writes=[("hid", f)])
        preload_ln()
        kgs = [(0, 16), (16, 16), (32, 12)]
        for dp in range(KD // 2):
            pbs = [bank() for _ in range(2)]
            for gi, (k0, kc) in enumerate(kgs):
                sd, kd = wload(pfx + "d", k0, kc, dp * 256)
                for j in range(2):
                    pb, pk = pbs[j]
                    for fl in range(kc):
                        f = k0 + fl
                        mm(pb[:, 0:T], sd[:, fl, j * 128:(j + 1) * 128], hid[:, f, :], f == 0, f == KF - 1,
                           [kd, ("hid", f)], pk)
            for j in range(2):
                d = dp * 2 + j
                pb, pk = pbs[j]
                op("dve", lambda e, pb=pb, d=d: e.scalar_tensor_tensor(
                    out=xT[:, d, :], in0=pb[:, 0:T], scalar=0.5, in1=xT[:, d, :], op0=ALU.mult, op1=ALU.add),
                   reads=[pk, ("x", d)], writes=[("x", d)])

    def UQ(c):
        return [("uq", c, q) for q in range(NQ)]

    def mixer(tile):
        U = lambda n: ("u", n)
        rmsnorm(1)
        slabs_q = [wload("win", 0, KD, cb * 256) for cb in range(2)]
        banks_q = [bank() for _ in range(4)]
        for k in range(KD):
            for c in range(4):
                sw, kw = slabs_q[c // 2]
                pb, pk = banks_q[c]
                mm(pb[:, 0:T], sw[:, k, (c % 2) * 128:(c % 2 + 1) * 128], hT[:, k, :], k == 0, k == KD - 1, [kw, ("h", k)], pk)
        for c in range(4):
            pb, pk = banks_q[c]
            op("act", lambda e, pb=pb, c=c: e.mul(out=qz[:, c, :], in_=pb[:, 0:T], mul=0.125),
               reads=[pk], writes=UQ(c))
        for cb in range(2, 4):
            sw, kw = wload("win", 0, KD, cb * 256)
            for j in range(2):
                c = cb * 2 + j
                pb, pk = bank()
                for k in range(KD):
                    mm(pb[:, 0:T], sw[:, k, j * 128:(j + 1) * 128], hT[:, k, :], k == 0, k == KD - 1, [kw, ("h", k)], pk)
                op("act", lambda e, pb=pb, c=c: e.mul(out=qz[:, c, :], in_=pb[:, 0:T], mul=0.125),
                   reads=[pk], writes=UQ(c))
        for gp in range(2):
            sw, kw = wload("win", 0, KD, 1024 + gp * 128, dup=True)
            for j in range(2):
                g = gp * 2 + j
                pb, pk = bank()
                for k in range(KD):
                    mm(pb[:, 0:T], sw[:, k, j * 128:(j + 1) * 128], hT[:, k, :], k == 0, k == KD - 1, [kw, ("h", k)], pk)
                op("dve", lambda e, pb=pb, g=g: e.tensor_copy(out=kT2[:, g, 128:128 + T], in_=pb[:, 0:T]),
                   reads=[pk], writes=[("kT2", g)])
        sw, kw = wload("win", 0, KD, 1280)
        for q in range(NQ):
            pb, pk = bank()
            for k in range(KD):
                mm(pb[:, 0:256], hT[:, k, q * 128:(q + 1) * 128], sw[:, k, :], k == 0, k == KD - 1, [kw, ("h", k)], pk)
            op("act", lambda e, pb=pb, q=q: e.copy(out=vbuf[:, 1 + q, :], in_=pb[:, 0:256]),
               reads=[pk], writes=[("vb", 1 + q)])

        fillers = []
        ustate = {}

        def u_unit(c):
            def f():
                cb, j = c // 2, c % 2
                if j == 0:
                    ustate["u"] = wload("win", 0, KD, 1536 + cb * 256)
                sw, kw = ustate["u"]
                pb, pk = bank()
                for k in range(KD):
                    mm(pb[:, 0:T], sw[:, k, j * 128:(j + 1) * 128], hT[:, k, :], k == 0, k == KD - 1, [kw, ("h", k)], pk)
                op("act", lambda e: e.activation(out=guT[:, c, :], in_=pb[:, 0:T], func=AF.Gelu_apprx_tanh),
                   reads=[pk], writes=[U(16 + c)])
            return f

        def z_unit(zs, q):
            def f():
                if q == 0:
                    ustate["z"] = wload("win", 0, KD, 2560 + zs * 256)
                sw, kw = ustate["z"]
                pb, pk = bank()
                for k in range(KD):
                    mm(pb[:, 0:256], hT[:, k, q * 128:(q + 1) * 128], sw[:, k, :], k == 0, k == KD - 1, [kw, ("h", k)], pk)
                op("act", lambda e: e.activation(out=gz[:, q, zs * 256:(zs + 1) * 256], in_=pb[:, 0:256], func=AF.Gelu_apprx_tanh),
                   reads=[pk], writes=[U(24 + 4 * q + zs)])
            return f

        def ln_stats():
            st_, stk = lnst[:, 0:32], "lnst"
            ustate["st"] = (st_, stk)
            for q in range(NQ):
                bn_, bnk = sm()
                for hh in range(2):
                    op("dve", lambda e, q=q, hh=hh, bn_=bn_: e.bn_stats(out=bn_[:, hh * 6:(hh + 1) * 6], in_=gz[:, q, hh * 512:(hh + 1) * 512]),
                       reads=[U(24 + 4 * q + 2 * hh), U(24 + 4 * q + 2 * hh + 1)], writes=[bnk])
                op("dve", lambda e, q=q, bn_=bn_: e.bn_aggr(out=st_[:, 2 * q:2 * q + 2], in_=bn_[:, 0:12].rearrange("p (a b) -> p a b", a=2)),
                   reads=[bnk], writes=[stk])
            var_ap = st_[:, 0:2 * NQ].rearrange("p (q t) -> p q t", t=2)[:, :, 1]
            rs_, rsk = lnst[:, 32:64], "lnrs"
            ustate["rs"] = (rs_, rsk)
            rstd_from(var_ap, [stk], NQ, 1.0, rs_[:, 0:NQ], rsk)

        def ln_unit(q):
            def f():
                st_, stk = ustate["st"]
                rs_, rsk = ustate["rs"]
                gk = [U(24 + 4 * q + z) for z in range(4)]
                op("dve", lambda e: e.tensor_scalar(out=zl[:, q, :], in0=gz[:, q, :], scalar1=st_[:, 2 * q:2 * q + 1],
                                                    scalar2=rs_[:, q:q + 1], op0=ALU.subtract, op1=ALU.mult),
                   reads=gk + [stk, rsk], writes=[U(8 + 2 * q), U(9 + 2 * q)])
            return f

        def sgu_unit(j):
            def f():
                pb, pk = bank()
                p3 = pb[:, 0:T].rearrange("p (q t) -> p q t", t=128)
                for q in range(NQ):
                    for gi in range(2):
                        g = 2 * j + gi
                        mm(p3[gi * 64:(gi + 1) * 64, q, :], zl[:, q, g * 64:(g + 1) * 64], wsT[:, g, :], True, True,
                           [U(8 + 2 * q), U(9 + 2 * q), "wsT"], pk)
                t_, tk = t2()
                t3 = t_[:, 0:T].rearrange("p (q t) -> p q t", t=128)
                op("dve", lambda e: e.scalar_tensor_tensor(
                    out=t3, in0=p3, scalar=lngb[:, j:j + 1], in1=bsb[:, j, :].unsqueeze(1).to_broadcast([128, NQ, 128]),
                    op0=ALU.mult, op1=ALU.add),
                   reads=[pk, "bsb2", "lngb"], writes=[tk])
                op("dve", lambda e: e.tensor_tensor(out=guT[:, j, :], in0=t_[:, 0:T], in1=guT[:, j, :], op=ALU.mult),
                   reads=[tk, U(16 + j)], writes=[U(16 + j)])
            return f

        for c in range(8):
            u_unit(c)()
        for zs in range(4):
            for q in range(NQ):
                fillers.append(z_unit(zs, q))
        fillers.append(ln_stats)
        for q in range(NQ):
            fillers.append(ln_unit(q))
        for j in range(8):
            fillers.append(sgu_unit(j))

        its = [(qb, g) for qb in range(NQ) for g in range(4)]
        ast = {}

        def att_A1(i):
            qb, g = its[i]
            psA, kA = bank()
            psB, kB = bank()
            s_, sk = t4()
            s3 = s_[:, :].rearrange("p (h k) -> p h k", k=256)
            for hh in range(4):
                hq = 4 * g + hh
                c, po = hq // 2, (hq % 2) * 64
                pp, pkk = (psA, kA) if hh % 2 == 0 else (psB, kB)
                col = (hh // 2) * 256
                mm(pp[:, col:col + 256], qz[po:po + 64, c, qb * 128:(qb + 1) * 128],
                   kT2[po:po + 64, g, qb * 128:qb * 128 + 256], True, True,
                   [("uq", c, qb), ("kT2", g), "kT2pre"], pkk)
            for hh in range(4):
                hq = 4 * g + hh
                pp, pkk = (psA, kA) if hh % 2 == 0 else (psB, kB)
                col = (hh // 2) * 256
                op("dve", lambda e, hh=hh, hq=hq, pp=pp, col=col: e.scalar_tensor_tensor(
                    out=s3[:, hh, :], in0=negdist[:, :], scalar=SLOPES[hq], in1=pp[:, col:col + 256],
                    op0=ALU.mult, op1=ALU.add),
                   reads=[pkk, "negdist"], writes=[sk])
            if tile == 0 and qb == 0:
                op("dve", lambda e: e.memset(s3[:, :, 0:128], -1e30), writes=[sk])
            m_, mk = sm()
            r_, rk = sm()
            op("dve", lambda e: e.tensor_reduce(out=m_[:, 0:4], in_=s3, axis=AX.X, op=ALU.max),
               reads=[sk], writes=[mk])
            op("dve", lambda e: e.tensor_tensor(out=m_[:, 0:4], in0=m_[:, 0:4], in1=sinkb[:, 4 * g:4 * g + 4], op=ALU.max),
               reads=[mk, "sinkb"], writes=[mk])
            op("dve", lambda e: e.tensor_scalar(out=m_[:, 4:8], in0=m_[:, 0:4], scalar1=-1.0, scalar2=None, op0=ALU.mult),
               reads=[mk], writes=[mk])
            op("dve", lambda e: e.tensor_tensor(out=r_[:, 4:8], in0=sinkb[:, 4 * g:4 * g + 4], in1=m_[:, 4:8], op=ALU.add),
               reads=[mk, "sinkb"], writes=[rk])
            op("dve", lambda e: e.memset(r_[:, 0:4], 0.0), writes=[rk])
            ast[i] = {"s3": s3, "sk": sk, "m": m_, "mk": mk, "r": r_, "rk": rk}

        def att_A2(i):
            st = ast[i]
            s3, sk, m_, mk, r_, rk = st["s3"], st["sk"], st["m"], st["mk"], st["r"], st["rk"]
            for hh in range(4):
                op("act", lambda e, hh=hh: e.activation(
                    out=s3[:, hh, :], in_=s3[:, hh, :], func=AF.Exp, bias=m_[:, 4 + hh:5 + hh], scale=1.0,
                    accum_out=r_[:, hh:hh + 1]),
                   reads=[sk, mk, rk], writes=[sk, rk])
            op("act", lambda e: e.activation(out=r_[:, 4:8], in_=r_[:, 4:8], func=AF.Exp),
               reads=[rk], writes=[rk])
            op("dve", lambda e: e.tensor_tensor(out=r_[:, 8:12], in0=r_[:, 0:4], in1=r_[:, 4:8], op=ALU.add),
               reads=[rk], writes=[rk])
            op("dve", lambda e: e.reciprocal(out=r_[:, 12:16], in_=r_[:, 8:12]), reads=[rk], writes=[rk])

        def att_A3(i):
            st = ast[i]
            s3, sk, r_, rk = st["s3"], st["sk"], st["r"], st["rk"]
            p_, pk_ = t2()
            P3 = p_[:, :].bitcast(BF16).rearrange("p (h k) -> p h k", k=256)
            for hh in range(4):
                op("act", lambda e, hh=hh: e.mul(out=P3[:, hh, :], in_=s3[:, hh, :], mul=r_[:, 12 + hh:13 + hh]),
                   reads=[sk, rk], writes=[pk_])
            st["P3"], st["pk"] = P3, pk_

        def att_B1(i):
            P3, pk_ = ast[i]["P3"], ast[i]["pk"]
            pt, ptk = bank()
            ptb = pt[:, :].bitcast(BF16).rearrange("p (kb h q) -> p kb h q", kb=2, h=4)
            for kb in range(2):
                for hh in range(4):
                    op("pe", lambda e, kb=kb, hh=hh: e.transpose(
                        out=ptb[:, kb, hh, :], in_=P3[:, hh, kb * 128:(kb + 1) * 128], identity=identb[:]),
                       reads=[pk_, "identb"], writes=[ptk])
            pts_, ptsk = t2()
            ptsb = pts_[:, :].bitcast(BF16)
            op("dve", lambda e: e.tensor_copy(out=ptsb, in_=pt[:, :].bitcast(BF16)), reads=[ptk], writes=[ptsk])
            ast[i]["pts4"] = ptsb.rearrange("p (kb h q) -> p kb h q", kb=2, h=4)
            ast[i]["ptsk"] = ptsk

        def att_B2(i):
            qb, g = its[i]
            pts4, ptsk = ast[i]["pts4"], ast[i]["ptsk"]
            po_, pok = bank()
            o3 = po_[:, 0:256].rearrange("p (hp q) -> p hp q", q=128)
            for hh in range(4):
                hp, half = hh // 2, hh % 2
                for kb in range(2):
                    mm(o3[half * 64:(half + 1) * 64, hp, :], vbuf[:, qb + kb, g * 64:(g + 1) * 64],
                       pts4[:, kb, hh, :], kb == 0, kb == 1, [("vb", qb + kb), ptsk], pok)
            op("act", lambda e: e.copy(out=qz[:, 2 * g:2 * g + 2, qb * 128:(qb + 1) * 128], in_=o3),
               reads=[pok], writes=[("uq", 2 * g, qb), ("uq", 2 * g + 1, qb)])
            del ast[i]

        nit = len(its)
        stages_att = [att_A1, att_A2, att_A3, att_B1, att_B2]
        nsteps = nit + len(stages_att) - 1
        fi = 0
        for step in range(nsteps):
            for lag, fn in enumerate(stages_att):
                i = step - lag
                if 0 <= i < nit:
                    fn(i)
            nz = 4 * NQ + 1
            if step < 4:
                want = (nz * (step + 1) + 3) // 4
            else:
                want = nz + ((len(fillers) - nz) * (step - 3) + (nsteps - 8) - 1) // max(1, nsteps - 8)
            while fi < min(want, len(fillers)):
                fillers[fi]()
                fi += 1
        while fi < len(fillers):
            fillers[fi]()
            fi += 1
        op("dve", lambda e: e.tensor_copy(out=kT2[:, :, 0:128], in_=kT2[:, :, T:T + 128]),
           reads=[("kT2", g) for g in range(4)], writes=["kT2pre"])
        op("dve", lambda e: e.tensor_copy(out=vbuf[:, 0, :], in_=vbuf[:, NQ, :]), reads=[("vb", NQ)], writes=[("vb", 0)])
        for dp in range(8):
            i = cnt["slot"] % NSLOT
            cnt["slot"] += 1
            sab, kab = ring[i], ("slot", i)
            op("pool", lambda e, sab=sab, dp=dp: e.dma_start(out=sab[:, 0:8, :], in_=W["wpa"][:, 0:8, dp * 256:(dp + 1) * 256]),
               writes=[kab], dma_key=kab)
            op("pool", lambda e, sab=sab, dp=dp: e.dma_start(out=sab[:, 8:16, :], in_=W["wpb"][:, 0:8, dp * 256:(dp + 1) * 256]),
               writes=[kab], dma_key=kab)
            sga, kga = wload("wbg", 0, KD, dp * 256)
            sgb, kgb = wload("wbg", 0, KD, 2048 + dp * 256)
            for j in range(2):
                d = dp * 2 + j
                cs = slice(j * 128, (j + 1) * 128)
                pGA, pGAk = bank()
                pGB, pGBk = bank()
                pA, pAk = bank()
                pB, pBk = bank()

                def g_groups():
                    for k in range(KD):
                        mm(pGA[:, 0:T], sga[:, k, cs], hT[:, k, :], k == 0, k == KD - 1, [kga, ("h", k)], pGAk)
                    for k in range(KD):
                        mm(pGB[:, 0:T], sgb[:, k, cs], hT[:, k, :], k == 0, k == KD - 1, [kgb, ("h", k)], pGBk)

                def p_groups():
                    for c in range(8):
                        mm(pA[:, 0:T], sab[:, c, cs], qz[:, c, :], c == 0, c == 7, [kab] + UQ(c), pAk)
                    for c in range(8):
                        mm(pB[:, 0:T], sab[:, 8 + c, cs], guT[:, c, :], c == 0, c == 7, [kab, ("u", 16 + c)], pBk)

                if j == 0:
                    g_groups()
                    p_groups()
                else:
                    p_groups()
                    g_groups()
                ga_, gak = t2()
                gb_, gbk = t2()
                op("act", lambda e, ga_=ga_, pGA=pGA, d=d: e.activation(out=ga_[:, 0:T], in_=pGA[:, 0:T], func=AF.Sigmoid, bias=bgate[:, d:d + 1], scale=1.0),
                   reads=[pGAk, "bgate"], writes=[gak])
                op("act", lambda e, gb_=gb_, pGB=pGB, d=d: e.activation(out=gb_[:, 0:T], in_=pGB[:, 0:T], func=AF.Sigmoid, bias=bgate[:, 16 + d:17 + d], scale=1.0),
                   reads=[pGBk, "bgate"], writes=[gbk])
                op("dve", lambda e, ga_=ga_, pA=pA: e.tensor_tensor(out=ga_[:, 0:T], in0=ga_[:, 0:T], in1=pA[:, 0:T], op=ALU.mult),
                   reads=[gak, pAk], writes=[gak])
                op("dve", lambda e, gb_=gb_, pB=pB: e.tensor_tensor(out=gb_[:, 0:T], in0=gb_[:, 0:T], in1=pB[:, 0:T], op=ALU.mult),
                   reads=[gbk, pBk], writes=[gbk])
                op("dve", lambda e, ga_=ga_, gb_=gb_, d=d: e.tensor_tensor(out=mrg[:, d, :], in0=ga_[:, 0:T], in1=gb_[:, 0:T], op=ALU.add),
                   reads=[gak, gbk], writes=[("u", 24 + d)])
        preload_ln()
        for dp in range(8):
            so, ko = wload("wout", 0, KD, dp * 256)
            for j in range(2):
                d = dp * 2 + j
                pb, pk = bank()
                for k in range(KD):
                    mm(pb[:, 0:T], so[:, k, j * 128:(j + 1) * 128], mrg[:, k, :], k == 0, k == KD - 1, [ko, ("u", 24 + k)], pk)
                op("dve", lambda e, pb=pb, d=d: e.tensor_tensor(out=xT[:, d, :], in0=pb[:, 0:T], in1=xT[:, d, :], op=ALU.add),
                   reads=[pk, ("x", d)], writes=[("x", d)])

    out_ops = []
    final = "final" in stages
    cin = {"n": 0}

    def stg_keys(i):
        return [("stg", i)] + [("hid", f) for f in range(8 * i, 8 * i + 8)]

    def x_load(t0, q):
        i = 2 + cin["n"] % 3
        cin["n"] += 1
        sg_ = stg[i]
        op("sp", lambda e: e.dma_start(out=sg_, in_=x_d[t0 + q * 128:t0 + (q + 1) * 128, :]),
           writes=stg_keys(i), dma_key=("stg", i))
        return i

    def x_in(q, i):
        sg_ = stg[i]
        for kk in range(KD // 4):
            pb, pk = bank()
            for j in range(4):
                k = kk * 4 + j
                op("pe", lambda e, pb=pb, j=j, k=k: e.transpose(
                    out=pb[:, j * 128:(j + 1) * 128], in_=sg_[:, k * 128:(k + 1) * 128], identity=identf[:]),
                   reads=[("stg", i), "identf"], writes=[pk])
            wk = [("x", kk * 4 + j) for j in range(4)] + [("xq", kk * 4 + j, q) for j in range(4)]
            if kk % 2 == 0:
                op("act", lambda e, pb=pb, kk=kk: e.copy(
                    out=xT[:, kk * 4:(kk + 1) * 4, q * 128:(q + 1) * 128], in_=pb[:, :].rearrange("p (i t) -> p i t", t=128)),
                   reads=[pk], writes=wk)
            else:
                op("dve", lambda e, pb=pb, kk=kk: e.tensor_copy(
                    out=xT[:, kk * 4:(kk + 1) * 4, q * 128:(q + 1) * 128], in_=pb[:, :].rearrange("p (i t) -> p i t", t=128)),
                   reads=[pk], writes=wk)

    def y_out(t0, q, rt):
        io = q % 2
        sg_ = stg[io]
        for kk in range(KD // 4):
            pb, pk = bank()
            for j in range(4):
                k = kk * 4 + j
                op("pe", lambda e, pb=pb, j=j, k=k: e.transpose(
                    out=pb[:, j * 128:(j + 1) * 128], in_=xT[:, k, q * 128:(q + 1) * 128], identity=identf[:]),
                   reads=[("xq", k, q), "identf"], writes=[pk])
            if rt is not None:
                rt_, rtk = rt
                if kk % 2 == 0:
                    op("act", lambda e, pb=pb, kk=kk: e.mul(out=sg_[:, kk * 512:(kk + 1) * 512], in_=pb[:, :], mul=rt_[:, q:q + 1]),
                       reads=[pk, rtk], writes=stg_keys(io))
                else:
                    op("dve", lambda e, pb=pb, kk=kk: e.tensor_scalar(out=sg_[:, kk * 512:(kk + 1) * 512], in0=pb[:, :],
                                                                    scalar1=rt_[:, q:q + 1], scalar2=None, op0=ALU.mult),
                       reads=[pk, rtk], writes=stg_keys(io))
            elif kk % 2 == 0:
                op("act", lambda e, pb=pb, kk=kk: e.copy(out=sg_[:, kk * 512:(kk + 1) * 512], in_=pb[:, :]),
                   reads=[pk], writes=stg_keys(io))
            else:
                op("dve", lambda e, pb=pb, kk=kk: e.tensor_copy(out=sg_[:, kk * 512:(kk + 1) * 512], in_=pb[:, :]),
                   reads=[pk], writes=stg_keys(io))
        o = op("sp", lambda e: e.dma_start(out=y_d[t0 + q * 128:t0 + (q + 1) * 128, :], in_=sg_),
               reads=stg_keys(io), dma_key=("out", io))
        out_ops.append(o)

    pend = [x_load(0, q) for q in range(min(3, NQ))]
    for q in range(NQ):
        if q + 3 < NQ:
            pass
        x_in(q, pend[q])
        if q + 3 < NQ:
            pend.append(x_load(0, q + 3))
    for tile in range(ntiles):
        t0 = tile * T
        if "ffn1" in stages:
            ffn("f1", 0)
        if "mix" in stages:
            if tile == 0:
                setup_mixer_consts()
            mixer(tile)
        if "ffn2" in stages:
            ffn("f2", 2)
        rt = None
        if final:
            pst, pstk = bank()
            for k in range(KD):
                sq_, sk = t2()
                sq = sq_[:, :].bitcast(BF16)[:, 0:T]
                op("act", lambda e, k=k, sq=sq: e.activation(out=sq, in_=xT[:, k, :], func=AF.Square),
                   reads=[("x", k)], writes=[sk])
                for q in range(NQ):
                    op("pe", lambda e, k=k, q=q, sq=sq, pst=pst: e.matmul(
                        pst[:, q:q + 1], lhsT=sq[:, q * 128:(q + 1) * 128], rhs=onesb[:, 0:1],
                        start=(k == 0 and q == 0), stop=(k == KD - 1), skip_group_check=True),
                       reads=[sk, "onesb"], writes=[pstk])
                op("dve", lambda e, k=k: e.tensor_scalar(out=xT[:, k, :], in0=xT[:, k, :], scalar1=norms[:, 3 * KD + k:3 * KD + k + 1],
                                                         scalar2=None, op0=ALU.mult),
                   reads=[("x", k), "norms"], writes=[("x", k)] + [("xq", k, q) for q in range(NQ)])
            rt_, rtk = sm()
            rstd_from(pst[:, 0:NQ], [pstk], NQ, 1.0 / D, rt_[:, 0:NQ], rtk)
            rt = (rt_, rtk)
        else:
            for k in range(KD):
                op("dve", lambda e, k=k: e.tensor_copy(out=xT[:, k, 0:1], in_=xT[:, k, 0:1]),
                   reads=[("x", k)], writes=[("x", k)] + [("xq", k, q) for q in range(NQ)])
        nxt = tile + 1 < ntiles
        t1 = (tile + 1) * T
        pend = [x_load(t1, q) for q in range(min(3, NQ))] if nxt else []
        for q in range(NQ):
            y_out(t0, q, rt)
            if nxt and q >= 1:
                qi = q - 1
                x_in(qi, pend[qi])
                if qi + 3 < NQ:
                    pend.append(x_load(t1, qi + 3))
        if nxt:
            qi = NQ - 1
            x_in(qi, pend[qi])
    Sx.emit(final_waits=out_ops)
    return nc


_NC_CACHE = {}


def _consts():
    i = np.arange(128, dtype=np.float64)[:, None]
    j = np.arange(256, dtype=np.float64)[None, :]
    dist = 128.0 + i - j
    valid = (dist >= 0) & (dist < 128)
    negdist = np.where(valid, -dist, -1e32).astype(np.float32)
    t = np.arange(128)[:, None]
    s = np.arange(128)[None, :]
    tril = (s <= t).astype(np.float32)
    return np.eye(128, dtype=np.float32), negdist, tril


def _pk(v):
    return np.ascontiguousarray(np.asarray(v, dtype=np.float32).reshape(-1, 128).T)


def make_in_maps(inputs):
    ident, negdist, tril = _consts()
    f = lambda a: np.ascontiguousarray(np.asarray(a, dtype=np.float32))
    norms = np.concatenate([_pk(inputs["ffn1_norm"][0]), _pk(inputs["mix_norm"][0]),
                            _pk(inputs["ffn2_norm"][0]), _pk(inputs["final_norm"])], axis=1)
    shared = {
        "norms": np.ascontiguousarray(norms),
        "bgate": _pk(inputs["b_branch_gate"][0]),
        "sinks": f(inputs["attn_sinks"]).reshape(1, 16),
        "lng": _pk(inputs["sgu_norm_g"][0]),
        "lnb": _pk(inputs["sgu_norm_b"][0]),
        "ws": f(inputs["sgu_w_s"][0]),
        "bs": f(inputs["sgu_b_s"][0]),
        "ident": ident, "negdist": negdist, "tril": tril,
        "f1g": f(inputs["ffn1_w_gate"][0]), "f1u": f(inputs["ffn1_w_up"][0]), "f1d": f(inputs["ffn1_w_down"][0]),
        "win": f(inputs["w_in"][0]), "wpa": f(inputs["w_proj_attn"][0]), "wpb": f(inputs["w_proj_sgu"][0]),
        "wbg": f(inputs["w_branch_gate"][0]), "wout": f(inputs["w_out"][0]),
        "f2g": f(inputs["ffn2_w_gate"][0]), "f2u": f(inputs["ffn2_w_up"][0]), "f2d": f(inputs["ffn2_w_down"][0]),
    }
    x = np.asarray(inputs["x"], dtype=np.float32)
    maps = []
    for c in range(NCORES):
        m = dict(shared)
        m["x"] = np.ascontiguousarray(x[c])
        maps.append(m)
    return maps


def kernel(**inputs):
    key = "full"
    if key not in _NC_CACHE:
        _NC_CACHE[key] = build_nc(stages=("ffn1", "mix", "ffn2", "final"))
    nc = _NC_CACHE[key]
    in_maps = make_in_maps(inputs)
    res = run_bass_kernel_spmd(nc, in_maps, core_ids=list(range(NCORES)))
    out = np.stack([np.asarray(r["y"], dtype=np.float32) for r in res.results], axis=0)
    return out
```
